# Optimizing a Trainium2 kernel written in Bass

```python
import math
import jax, jax.numpy as jnp
from jax import lax
import numpy as np

D_MODEL = 2048
BATCH = 16
SEQ = 2048
DEPTH = 4

MEM_LEN = 256
XA_HEADS = 4
XA_HEAD_DIM = D_MODEL // XA_HEADS
MIX_WIDTH = D_MODEL
POOL_WIDTH = MIX_WIDTH // 2
POOL_WINDOWS = (2, 4, 8, 16)
POOL_GROUP = POOL_WIDTH // len(POOL_WINDOWS)
DN_HEAD_DIM = 128
DN_WIDTH = MIX_WIDTH - POOL_WIDTH
DN_HEADS = DN_WIDTH // DN_HEAD_DIM
DN_CONV = 4
DN_CHUNK = 64
IN_WIDTH = POOL_WIDTH + 4 * DN_WIDTH + 2 * DN_HEADS
D_FF = 256 * ((8 * D_MODEL // 3 + 255) // 256)
FFN_CONV = 3
EPS = 1e-6

kernel_name = "hybrid_pool_deltanet_memxattn_convglu"


def rms_norm(x, g):
    xf = x.astype(jnp.float32)
    y = xf * lax.rsqrt(jnp.mean(xf * xf, axis=-1, keepdims=True) + EPS)
    return (y * g.astype(jnp.float32)).astype(x.dtype)


def l2_normalize(t):
    return t * lax.rsqrt(jnp.sum(t * t, axis=-1, keepdims=True) + EPS)


def causal_dwconv(x, w, b=None):
    K = w.shape[0]
    S = x.shape[1]
    xp = jnp.pad(x, ((0, 0), (K - 1, 0), (0, 0)))
    y = sum(xp[:, k:k + S] * w[k] for k in range(K))
    if b is not None:
        y = y + b
    return y


def pool_mixer(u, w_pool, pool_scale):
    B, S, _ = u.shape
    uf = u.astype(jnp.float32)
    cs = jnp.pad(jnp.cumsum(uf, axis=1), ((0, 0), (1, 0), (0, 0)))
    pos = jnp.arange(1, S + 1, dtype=jnp.float32)
    groups = []
    for i, w in enumerate(POOL_WINDOWS):
        sl = slice(i * POOL_GROUP, (i + 1) * POOL_GROUP)
        c = cs[:, :, sl]
        lagged = jnp.pad(c, ((0, 0), (w - 1, 0), (0, 0)))[:, :S]
        count = jnp.minimum(pos, float(w))
        mean = (c[:, 1:] - lagged) / count[None, :, None]
        groups.append(mean - uf[:, :, sl])
    mixed = jnp.stack(groups, axis=2).astype(u.dtype)
    y = jnp.einsum('bsng,ngh->bsnh', mixed, w_pool).reshape(B, S, POOL_WIDTH)
    return y * pool_scale


def gated_delta_net(q, k, v, z, b_logit, a_logit, conv_w, a_log, dt_bias, norm_g):
    B, S, _ = q.shape
    H, Dh, C = DN_HEADS, DN_HEAD_DIM, DN_CHUNK
    N = S // C
    f32 = jnp.float32
    qkv = jax.nn.silu(causal_dwconv(jnp.concatenate([q, k, v], axis=-1), conv_w)).astype(f32)
    q, k, v = [t.reshape(B, S, H, Dh) for t in jnp.split(qkv, 3, axis=-1)]
    q = l2_normalize(q) * (Dh ** -0.5)
    k = l2_normalize(k)
    beta = jax.nn.sigmoid(b_logit.astype(f32))
    g = -jnp.exp(a_log.astype(f32)) * jax.nn.softplus(a_logit.astype(f32) + dt_bias.astype(f32))

    def to_chunks(t):
        return t.reshape(B, N, C, H, -1).transpose(0, 3, 1, 2, 4)

    q, k, v = to_chunks(q), to_chunks(k), to_chunks(v)
    beta = to_chunks(beta[..., None])[..., 0]
    gc = jnp.cumsum(to_chunks(g[..., None])[..., 0], axis=-1)
    idx = jnp.arange(C)
    causal = idx[:, None] >= idx[None, :]
    strict = idx[:, None] > idx[None, :]
    decay = jnp.exp(jnp.where(causal, gc[..., :, None] - gc[..., None, :], -jnp.inf))
    kk = jnp.einsum('bhncd,bhnjd->bhncj', k, k)
    l_mat = jnp.where(strict, beta[..., :, None] * kk * decay, 0.0)
    rhs = jnp.concatenate([v * beta[..., None], k * (beta * jnp.exp(gc))[..., None]], axis=-1)
    sol = lax.linalg.triangular_solve(l_mat, rhs, left_side=True, lower=True, unit_diagonal=True)
    u_c, w_c = sol[..., :Dh], sol[..., Dh:]
    attn = jnp.einsum('bhncd,bhnjd->bhncj', q, k) * decay
    q_dec = q * jnp.exp(gc)[..., None]
    k_dec = k * jnp.exp(gc[..., -1:] - gc)[..., None]
    chunk_decay = jnp.exp(gc[..., -1])
    xs = tuple(jnp.moveaxis(t, 2, 0) for t in (u_c, w_c, q_dec, k_dec, attn, chunk_decay))

    def step(state, inp):
        u_i, w_i, qd_i, kd_i, a_i, dec_i = inp
        v_new = u_i - jnp.einsum('bhcd,bhde->bhce', w_i, state)
        o_i = jnp.einsum('bhcd,bhde->bhce', qd_i, state) + jnp.einsum('bhcj,bhje->bhce', a_i, v_new)
        state = state * dec_i[..., None, None] + jnp.einsum('bhcd,bhce->bhde', kd_i, v_new)
        return state, o_i

    _, o = lax.scan(step, jnp.zeros((B, H, Dh, Dh), f32), xs)
    o = o.transpose(1, 0, 3, 2, 4).reshape(B, S, H, Dh)
    o = rms_norm(o, norm_g) * jax.nn.silu(z.astype(f32).reshape(B, S, H, Dh))
    return o.reshape(B, S, DN_WIDTH).astype(z.dtype)


def hybrid_mixer(h, w_in, w_pool, pool_scale, dn_conv_w, dn_a_log, dn_dt_bias, dn_norm_g, w_mix_out):
    p = h @ w_in
    o0 = POOL_WIDTH
    u = p[..., :o0]
    q = p[..., o0:o0 + DN_WIDTH]
    k = p[..., o0 + DN_WIDTH:o0 + 2 * DN_WIDTH]
    v = p[..., o0 + 2 * DN_WIDTH:o0 + 3 * DN_WIDTH]
    z = p[..., o0 + 3 * DN_WIDTH:o0 + 4 * DN_WIDTH]
    o1 = o0 + 4 * DN_WIDTH
    b_logit = p[..., o1:o1 + DN_HEADS]
    a_logit = p[..., o1 + DN_HEADS:o1 + 2 * DN_HEADS]
    y_pool = pool_mixer(u, w_pool, pool_scale)
    y_dn = gated_delta_net(q, k, v, z, b_logit, a_logit, dn_conv_w, dn_a_log, dn_dt_bias, dn_norm_g)
    return jnp.concatenate([y_pool, y_dn], axis=-1) @ w_mix_out


def memory_cross_attention(h, mem_h, w_xq, w_xkv, w_xo):
    B, S, _ = h.shape
    M = mem_h.shape[1]
    q = (h @ w_xq).reshape(B, S, XA_HEADS, XA_HEAD_DIM)
    kv = mem_h @ w_xkv
    k = kv[..., :D_MODEL].reshape(B, M, XA_HEADS, XA_HEAD_DIM)
    v = kv[..., D_MODEL:].reshape(B, M, XA_HEADS, XA_HEAD_DIM)
    s = jnp.einsum('bshd,bmhd->bhsm', q, k).astype(jnp.float32) * (XA_HEAD_DIM ** -0.5)
    pr = jax.nn.softmax(s, axis=-1).astype(v.dtype)
    o = jnp.einsum('bhsm,bmhd->bshd', pr, v).reshape(B, S, D_MODEL)
    return o @ w_xo


def conv_glu_ffn(h, w_gate, w_up, conv_w, conv_b, w_down):
    gate = causal_dwconv(h @ w_gate, conv_w, conv_b)
    return (jax.nn.silu(gate) * (h @ w_up)) @ w_down


def setup_inputs(seed: int = 0) -> dict:
    key = jax.random.key(seed)
    ks = jax.random.split(key, 32)
    f32 = jnp.float32

    def nrm(i, shape, scale):
        return jax.random.normal(ks[i], shape, f32) * scale

    def gain(i, shape):
        return 1.0 + 0.02 * jax.random.normal(ks[i], shape, f32)

    dt = jnp.exp(jax.random.uniform(ks[7], (DEPTH, DN_HEADS), f32) * (math.log(0.1) - math.log(0.001)) + math.log(0.001))
    return {
        "x": nrm(0, (BATCH, SEQ, D_MODEL), 1.0),
        "mem": nrm(1, (BATCH, MEM_LEN, D_MODEL), 1.0),
        "mix_norm_g": gain(2, (DEPTH, D_MODEL)),
        "w_in": nrm(3, (DEPTH, D_MODEL, IN_WIDTH), D_MODEL ** -0.5),
        "w_pool": nrm(4, (DEPTH, len(POOL_WINDOWS), POOL_GROUP, POOL_GROUP), POOL_GROUP ** -0.5),
        "pool_scale": gain(5, (DEPTH, POOL_WIDTH)),
        "dn_conv_w": nrm(6, (DEPTH, DN_CONV, 3 * DN_WIDTH), DN_CONV ** -0.5),
        "dn_a_log": jnp.log(jax.random.uniform(ks[8], (DEPTH, DN_HEADS), f32, 1.0, 16.0)),
        "dn_dt_bias": dt + jnp.log(-jnp.expm1(-dt)),
        "dn_norm_g": gain(9, (DEPTH, DN_HEAD_DIM)),
        "w_mix_out": nrm(10, (DEPTH, MIX_WIDTH, D_MODEL), MIX_WIDTH ** -0.5),
        "xa_norm_g": gain(11, (DEPTH, D_MODEL)),
        "mem_norm_g": gain(12, (D_MODEL,)),
        "w_xq": nrm(13, (DEPTH, D_MODEL, D_MODEL), D_MODEL ** -0.5),
        "w_xkv": nrm(14, (DEPTH, D_MODEL, 2 * D_MODEL), D_MODEL ** -0.5),
        "w_xo": nrm(15, (DEPTH, D_MODEL, D_MODEL), D_MODEL ** -0.5),
        "ffn_norm_g": gain(16, (DEPTH, D_MODEL)),
        "w_gate": nrm(17, (DEPTH, D_MODEL, D_FF), D_MODEL ** -0.5),
        "w_up": nrm(18, (DEPTH, D_MODEL, D_FF), D_MODEL ** -0.5),
        "ffn_conv_w": nrm(19, (DEPTH, FFN_CONV, D_FF), FFN_CONV ** -0.5),
        "ffn_conv_b": nrm(20, (DEPTH, D_FF), 0.01),
        "w_down": nrm(21, (DEPTH, D_FF, D_MODEL), D_FF ** -0.5),
        "final_norm_g": gain(22, (D_MODEL,)),
    }


def reference(x, mem, mix_norm_g, w_in, w_pool, pool_scale, dn_conv_w, dn_a_log, dn_dt_bias, dn_norm_g,
              w_mix_out, xa_norm_g, mem_norm_g, w_xq, w_xkv, w_xo, ffn_norm_g, w_gate, w_up,
              ffn_conv_w, ffn_conv_b, w_down, final_norm_g):
    mem_h = rms_norm(mem, mem_norm_g)
    for l in range(DEPTH):
        x = x + hybrid_mixer(rms_norm(x, mix_norm_g[l]), w_in[l], w_pool[l], pool_scale[l], dn_conv_w[l],
                             dn_a_log[l], dn_dt_bias[l], dn_norm_g[l], w_mix_out[l])
        x = x + memory_cross_attention(rms_norm(x, xa_norm_g[l]), mem_h, w_xq[l], w_xkv[l], w_xo[l])
        x = x + conv_glu_ffn(rms_norm(x, ffn_norm_g[l]), w_gate[l], w_up[l], ffn_conv_w[l], ffn_conv_b[l], w_down[l])
    return rms_norm(x, final_norm_g)
```

```python
import contextlib
import numpy as np
import concourse.bass as bass
import concourse.mybir as mybir
from concourse.bass_utils import run_bass_kernel_spmd

F32 = mybir.dt.float32
BF16 = mybir.dt.bfloat16
AF = mybir.ActivationFunctionType
ALU = mybir.AluOpType

SEM_LIMIT = 30000
DMA_POOL = 8

D = 2048
NCH = 16
INW = 5136
DFF = 5632
MEM = 256
SEQ = 2048
NEG = -30000.0
EPS = 1e-6

P_MIXG, P_XAG, P_FFG, P_PS, P_DNW, P_DNG, P_FCW, P_FCB, P_DTB, P_ALOG, NCOL = 0, 16, 32, 48, 56, 152, 153, 285, 329, 330, 332


class Tok:
    __slots__ = ("name", "lw", "rd")

    def __init__(self, name=""):
        self.name = name
        self.lw = None
        self.rd = []


class TV:
    __slots__ = ("ap", "tok")

    def __init__(self, ap, tok):
        self.ap = ap
        self.tok = tok

    def __getitem__(self, idx):
        return TV(self.ap[idx], self.tok)

    def re(self, s, **kw):
        return TV(self.ap.rearrange(s, **kw), self.tok)


class Op:
    __slots__ = ("eng", "fn", "deps", "sig", "sem", "val", "idx", "isdma", "prev")

    def __init__(self, eng, fn, isdma):
        self.eng = eng
        self.fn = fn
        self.deps = []
        self.sig = False
        self.sem = None
        self.val = 0
        self.isdma = isdma
        self.prev = None


class Ring:
    def __init__(self, items):
        self.items = items
        self.i = 0

    def next(self):
        it = self.items[self.i % len(self.items)]
        self.i += 1
        return it


class K:
    ENGS = ("pe", "act", "dve", "pool", "sp")

    def __init__(self, nc):
        self.nc = nc
        self.ops = {e: [] for e in self.ENGS}
        self.es = contextlib.ExitStack()
        self.nops = 0

    def tile(self, name, shape, dt, n=None, psum=False):
        cm = self.nc.psum_tensor(name, shape, dt) if psum else self.nc.sbuf_tensor(name, shape, dt)
        t = self.es.enter_context(cm)
        if n is None:
            return TV(t[:], Tok(name))
        return [TV(t[:, i], Tok(f"{name}{i}")) for i in range(n)]

    def add(self, eng, fn, reads, writes, isdma=False):
        op = Op(eng, fn, isdma)
        op.idx = self.nops
        self.nops += 1
        deps = set()
        rtoks = [r.tok for r in reads if isinstance(r, TV)]
        for t in rtoks:
            if t.lw is not None:
                deps.add(t.lw)
        for w in writes:
            t = w.tok
            if t.lw is not None:
                deps.add(t.lw)
            for rr in t.rd:
                deps.add(rr)
        for d in deps:
            if d is op:
                continue
            if d.eng == eng and not d.isdma and not isdma:
                if eng == "pe":
                    continue
                if not any(t.lw is d for t in rtoks) and not any(w.tok.lw is d for w in writes):
                    continue
            op.deps.append(d)
            d.sig = True
        for t in rtoks:
            t.rd.append(op)
        for w in writes:
            w.tok.lw = op
            w.tok.rd = []
        self.ops[eng].append(op)
        return op

    @staticmethod
    def _a(x):
        return x.ap if isinstance(x, TV) else x

    def mm(self, out, lhsT, rhs, start=True, stop=True, tp=None):
        o, l, r = out.ap, lhsT.ap, rhs.ap
        if tp is None:
            f = lambda e: e.matmul(o, lhsT=l, rhs=r, start=start, stop=stop)
        else:
            f = lambda e: e.matmul(o, lhsT=l, rhs=r, start=start, stop=stop, tile_position=tp)
        return self.add("pe", f, [lhsT, rhs], [out])

    def transpose(self, out, in_, ident):
        o, i, d = out.ap, in_.ap, ident.ap
        return self.add("pe", lambda e: e.transpose(o, i, d), [in_, ident], [out])

    def act(self, out, in_, func, bias=0.0, scale=1.0, accum=None):
        o, i = out.ap, in_.ap
        b, s = self._a(bias), self._a(scale)
        if accum is not None:
            ac = accum.ap
            f = lambda e: e.activation(o, i, func, bias=b, scale=s, accum_out=ac)
            wr = [out, accum]
        else:
            f = lambda e: e.activation(o, i, func, bias=b, scale=s)
            wr = [out]
        return self.add("act", f, [in_, bias, scale], wr)

    def tt(self, eng, out, in0, in1, op):
        o, a, b = out.ap, in0.ap, in1.ap
        return self.add(eng, lambda e: e.tensor_tensor(o, a, b, op), [in0, in1], [out])

    def ts(self, eng, out, in0, s1, s2, op0, op1=None):
        o, a = out.ap, in0.ap
        x1, x2 = self._a(s1), self._a(s2)
        if op1 is None:
            f = lambda e: e.tensor_single_scalar(o, a, x1, op0)
        else:
            f = lambda e: e.tensor_scalar(o, a, x1, x2, op0, op1)
        return self.add(eng, f, [in0, s1, s2], [out])

    def stt(self, eng, out, in0, scalar, in1, op0, op1):
        o, a, b = out.ap, in0.ap, in1.ap
        sc = self._a(scalar)
        return self.add(eng, lambda e: e.scalar_tensor_tensor(o, a, sc, b, op0, op1),
                        [in0, scalar, in1], [out])

    def copy(self, eng, out, in_):
        o, i = out.ap, in_.ap
        if eng == "act":
            return self.add(eng, lambda e: e.copy(o, i), [in_], [out])
        return self.add(eng, lambda e: e.tensor_copy(o, i), [in_], [out])

    def memset(self, eng, out, val):
        o = out.ap
        return self.add(eng, lambda e: e.memset(o, val), [], [out])

    def recip(self, out, in_):
        o, i = out.ap, in_.ap
        return self.add("dve", lambda e: e.reciprocal(o, i), [in_], [out])

    def dma(self, q, out, in_):
        o, i = out.ap, in_.ap
        return self.add(q, lambda e: e.dma_start(out=o, in_=i), [in_], [out], isdma=True)

    def finalize(self):
        nc, es = self.nc, self.es
        engobj = {"pe": "tensor", "act": "scalar", "dve": "vector", "pool": "gpsimd", "sp": "sync"}
        for e in self.ENGS:
            sems, cnt = [], 0
            dma_sems, dma_cnt, ndma = [], [], 0
            for op in self.ops[e]:
                if op.isdma:
                    if len(dma_sems) < DMA_POOL:
                        dma_sems.append(es.enter_context(nc.semaphore(f"d_{e}{len(dma_sems)}")))
                        dma_cnt.append(0)
                    j = ndma % DMA_POOL
                    ndma += 1
                    op.prev = (dma_sems[j], dma_cnt[j]) if dma_cnt[j] > 0 else None
                    dma_cnt[j] += 16
                    op.sem, op.val, op.sig = dma_sems[j], dma_cnt[j], True
                elif op.sig:
                    if not sems or cnt >= SEM_LIMIT:
                        sems.append(es.enter_context(nc.semaphore(f"c_{e}{len(sems)}")))
                        cnt = 0
                    cnt += 1
                    op.sem, op.val = sems[-1], cnt
        block = es.enter_context(nc.Block())
        for e in self.ENGS:
            ops = self.ops[e]
            if not ops:
                continue

            def emit(eng, ops=ops):
                seen = {}
                for op in ops:
                    waits = {}
                    cands = [(d.sem, d.val) for d in op.deps]
                    if op.isdma and op.prev is not None:
                        cands.append(op.prev)
                    for sem, val in cands:
                        key = id(sem)
                        if seen.get(key, 0) >= val:
                            continue
                        if key not in waits or waits[key][1] < val:
                            waits[key] = (sem, val)
                    for key, (sem, val) in waits.items():
                        eng.wait_ge(sem, val)
                        seen[key] = val
                    ins = op.fn(eng)
                    if op.isdma:
                        ins.then_inc(op.sem, 16)
                    elif op.sig:
                        ins.then_inc(op.sem, 1)
                last = {}
                for op in ops:
                    if op.isdma:
                        last[id(op.sem)] = (op.sem, op.val)
                for key, (sem, val) in last.items():
                    if seen.get(key, 0) < val:
                        eng.wait_ge(sem, val)

            getattr(block, engobj[e])(emit)
        es.close()


def build_program(NSEQ, NBLK, TB, L, final, dbg=False):
    nc = bass.Bass("TRN2", target_bir_lowering=False)
    NT = TB // 128
    NCK = TB // 64
    NTOK = NSEQ * NBLK * TB

    def din(name, shape, dt=F32):
        return nc.dram_tensor(name, shape, dt, kind="ExternalInput").ap()

    x_d = din("x", [NTOK, D])
    mem_d = din("mem", [NSEQ * MEM, D])
    WSHAPES = [("w_xkv", D, 2 * D), ("w_in", D, INW), ("w_pool", 1024, 256), ("w_mix_out", D, D), ("w_xq", D, D),
               ("w_xo", D, D), ("w_gate", D, DFF), ("w_up", D, DFF), ("w_down", DFF, D)]
    wf32, wb16 = {}, {}
    for nm, K_, N_ in WSHAPES:
        wf32[nm] = din(nm, [L, K_, N_])
        wb16[nm] = nc.dram_tensor(nm + "_b", [L, K_, N_], BF16, kind="Internal").ap()
    prm_d = din("prm", [L, 128, NCOL])
    gprm_d = din("gprm", [128, 32])
    cst_d = din("cst", [128, 448])
    sel_d = din("sel", [16, 16 * 128])
    out_d = nc.dram_tensor("out", [NTOK, D], F32, kind="ExternalOutput").ap()
    kvs_d = nc.dram_tensor("kvs", [L, 128, 8192], BF16, kind=KVS_KIND).ap() if "mem" in STAGES else None
    dbg_d = None
    if dbg:
        dbg_d = nc.dram_tensor("dbg", [4, 128, NCH * TB], F32, kind="ExternalOutput").ap()

    k = K(nc)
    wtok = Tok("w")

    def dr(ap, tok=None):
        return TV(ap, tok if tok is not None else wtok)

    kvs_tok = [Tok(f"kvs{l}") for l in range(L)]
    WT = {nm: [TV(wb16[nm][l], Tok(f"{nm}{l}")) for l in range(L)] for nm, _, _ in WSHAPES}

    def convert(nm, l, K_):
        for r in range(0, K_, 128):
            k.dma("pool", WT[nm][l][r:r + 128, :], dr(wf32[nm][l][r:r + 128, :]))

    for l in range(L):
        convert("w_xkv", l, D)
    for l in range(L):
        for nm, K_, N_ in WSHAPES[1:]:
            convert(nm, l, K_)
    w_in_d, w_pool_d, w_mo_d, w_xq_d = WT["w_in"], WT["w_pool"], WT["w_mix_out"], WT["w_xq"]
    w_xkv_d, w_xo_d, w_gate_d, w_up_d, w_down_d = WT["w_xkv"], WT["w_xo"], WT["w_gate"], WT["w_up"], WT["w_down"]
    out_tok = Tok("out")

    cst = k.tile("cst_s", [128, 448], F32)
    k.dma("sp", cst, dr(cst_d))
    ident, Mnat, MT, icnt = cst[:, 0:128], cst[:, 128:256], cst[:, 256:384], cst[:, 384:444]
    sel = k.tile("sel_s", [16, 16 * 128], F32)
    k.dma("sp", sel, dr(sel_d))
    ones = k.tile("ones", [128, 128], BF16)
    k.memset("dve", ones, 1.0)
    onesel = k.tile("onesel", [128, 17, 16], BF16)
    k.memset("dve", onesel, 0.0)
    for j_ in range(16):
        k.memset("dve", onesel[:, j_, j_:j_ + 1], 1.0)
    identb = k.tile("identb", [128, 128], BF16)
    k.copy("dve", identb, ident)
    gprm = k.tile("gprm_s", [128, 32], F32)
    k.dma("sp", gprm, dr(gprm_d))
    prm = []
    nea = []
    for l in range(L):
        p = k.tile(f"prm_s{l}", [128, NCOL], F32)
        k.dma("sp", p, dr(prm_d[l]))
        prm.append(p)
        ne = k.tile(f"nea{l}", [16, 1], F32)
        k.act(ne, p[0:16, P_ALOG:P_ALOG + 1], AF.Exp)
        k.ts("dve", ne, ne, -1.0, None, ALU.mult)
        nea.append(ne)

    S = [k.tile(f"S{l}", [128, 8, 128], F32, n=8) for l in range(L)]
    Sbf = k.tile("Sbf", [128, 8, 128], BF16, n=8)
    ptail = [k.tile(f"ptail{l}", [128, 8, 15], F32, n=8) for l in range(L)]
    ctail = [k.tile(f"ctail{l}", [128, 24, 3], F32, n=24) for l in range(L)]
    ftail = [k.tile(f"ftail{l}", [128, 44, 2], F32, n=44) for l in range(L)]

    xfm = k.tile("xfm", [128, NCH, TB], F32, n=NCH)
    h = k.tile("h", [128, NCH, TB], BF16, n=NCH)
    AR = k.tile("ar", [128, 22, TB], BF16, n=22)
    sz = k.tile("sz", [128, 8, TB], BF16, n=8)
    vh = k.tile("vh", [128, 8, TB], BF16, n=8)
    vtok = k.tile("vtok", [128, NT, 1024], BF16, n=NT)
    kdtok = k.tile("kdtok", [128, NT, 1024], BF16, n=NT)
    xst = k.tile("xst", [128, D], F32)
    wring = Ring(k.tile("wring", [128, 5, 4096], BF16, n=5))
    _pmain_full = k.tile("pmain", [128, 3, 512], F32, n=3, psum=True)
    pmain = Ring([t[:, 0:TB] for t in _pmain_full])
    _psm_full = k.tile("psm", [128, NPSM, 512], F32, n=NPSM, psum=True)
    psm = Ring(_psm_full)
    _pso = k.tile("pso", [128, 2, 512], F32, n=2, psum=True)
    pso = [_pso[i // 4][:, (i % 4) * 128:(i % 4 + 1) * 128] for i in range(8)]
    pso_full = _pso
    _six = list(_psm_full) + list(_pmain_full)
    GR = []
    for i_ in range(3):
        GR.append({"ps": Ring([_six[2 * i_], _six[2 * i_ + 1]]),
                   "sm": Ring(k.tile(f"gsm{i_}", [128, 12, 128], F32, n=12)),
                   "sb": Ring(k.tile(f"gsb{i_}", [128, 4, 128], BF16, n=4))})
    r0ring = Ring(k.tile("r0r", [128, 4, 128], BF16, n=4))
    scan_ps = Ring(_six)
    NSLOT = 3
    _slot_banks = list(_psm_full)
    SL = []
    for i_ in range(NSLOT):
        SL.append({"tmpf": Ring(k.tile(f"sl_t{i_}", [128, 3, TB + 16], F32, n=3)),
                   "stage": Ring(k.tile(f"sl_s{i_}", [128, 2, TB + 16], F32, n=2)),
                   "sq": k.tile(f"sl_q{i_}", [128, TB], BF16),
                   "bank": _slot_banks[i_]})

    def q4(bank, j):
        return bank[:, j * 128:(j + 1) * 128]

    sqr = Ring(k.tile("sqr", [128, 2, TB], BF16, n=2))
    rstd_t = k.tile("rstd_t", [128, TB], F32)
    p1t = [k.tile(f"p1t{t_}", [128, 8, 128], BF16, n=8) for t_ in range(NT)]
    p2t = [k.tile(f"p2t{t_}", [128, 8, 128], BF16, n=8) for t_ in range(NT)]
    kgt = [k.tile(f"kgt{t_}", [128, 8, 128], BF16, n=8) for t_ in range(NT)]
    qdt = [k.tile(f"qdt{t_}", [128, 8, 128], BF16, n=8) for t_ in range(NT)]
    decs = [k.tile(f"decs{t_}", [128, 8, 2], F32, n=8) for t_ in range(NT)]
    ba_r = Ring(k.tile("ba_r", [16, 7, TB], F32, n=7))
    pack = k.tile("pack", [128, TB], F32)
    k.memset("dve", pack, 0.0)
    colT = k.tile("colT", [128, NT, 96], F32, n=NT)
    ngc = k.tile("ngc", [128, NT, 8], F32, n=NT)
    tiny = Ring(k.tile("tiny", [128, 8, 2], F32, n=8))
    Ebuf = k.tile("Ebuf", [128, 2, TB], BF16, n=2)

    evac_i = [0]

    def evac_eng():
        evac_i[0] += 1
        return "act" if evac_i[0] % 2 == 0 else "dve"

    def wload(W2d, k0, kc, n0, ncols):
        slot = wring.next()
        view = TV(slot.ap[:, 0:kc * ncols].rearrange("p (c n) -> p c n", n=ncols), slot.tok)
        src = W2d[k0 * 128:(k0 + kc) * 128, n0:n0 + ncols].re("(c p) n -> p c n", p=128)
        k.dma("sp", view, src)
        return view

    def linear_fm(W2d, kch, n0, ncols, cb, colw=256):
        nk = len(kch)
        parts = []
        s = 0
        while s < nk:
            e = min(nk, s + 16) if nk <= 16 or nk > 22 else min(nk, s + 11)
            parts.append((s, e))
            s = e
        for g in range(0, ncols, colw):
            w = min(colw, ncols - g)
            tiles = [wload(W2d, s, e - s, n0 + g, w) for (s, e) in parts]
            for m in range(w // 128):
                ps = pmain.next()
                for pi, (s, e) in enumerate(parts):
                    for kk in range(s, e):
                        k.mm(ps, tiles[pi][:, kk - s, m * 128:(m + 1) * 128], kch[kk],
                             start=(kk == 0), stop=(kk == nk - 1))
                cb((n0 + g) // 128 + m, ps)

    def rmsnorm_fm(src, gcol, dst, dst_dt_scale=1.0):
        ps = pmain.next()
        for c in range(NCH):
            sq = sqr.next()
            k.act(sq, src[c], AF.Square)
            k.mm(ps, ones, sq, start=(c == 0), stop=(c == NCH - 1))
        k.act(rstd_t, ps, AF.Ln, bias=EPS, scale=1.0 / D)
        k.act(rstd_t, rstd_t, AF.Exp, scale=-0.5)
        for c in range(NCH):
            k.stt("dve", dst[c], src[c], gcol[:, c:c + 1], rstd_t, ALU.mult, ALU.mult)

    def residual_cb(c, ps):
        k.tt("dve", xfm[c], xfm[c], ps, ALU.add)

    def dump(i):
        if dbg_d is not None:
            for c in range(NCH):
                k.dma("sp", dr(dbg_d[i][:, c * TB:(c + 1) * TB], out_tok), xfm[c])

    def load_tokmajor_to_fm(src2d, ntok_tiles, dst_cb):
        for tt_ in range(ntok_tiles):
            k.dma("sp", xst, dr(src2d[tt_ * 128:(tt_ + 1) * 128, :]))
            for c4 in range(4):
                bank = psm.next()
                for j in range(4):
                    c = c4 * 4 + j
                    k.transpose(q4(bank, j), xst[:, c * 128:(c + 1) * 128], ident)
                eng = evac_eng()
                for j in range(4):
                    dst_cb(c4 * 4 + j, tt_, q4(bank, j), eng)

    def mem_setup(s):
        memf = [TV(t.ap[:, 0:MEM] if TB >= MEM else None, t.tok) for t in xfm]
        assert TB >= MEM

        def cb(c, tt_, ps, eng):
            k.copy(eng, memf[c][:, tt_ * 128:(tt_ + 1) * 128], ps)

        load_tokmajor_to_fm(mem_d[s * MEM:(s + 1) * MEM, :], MEM // 128, cb)
        ps = pmain.next()
        for c in range(NCH):
            sq = sqr.next()
            k.act(sq[:, 0:MEM], memf[c], AF.Square)
            k.mm(ps[:, 0:MEM], ones, sq[:, 0:MEM], start=(c == 0), stop=(c == NCH - 1))
        k.act(rstd_t[:, 0:MEM], ps[:, 0:MEM], AF.Ln, bias=EPS, scale=1.0 / D)
        k.act(rstd_t[:, 0:MEM], rstd_t[:, 0:MEM], AF.Exp, scale=-0.5)
        mh = [t[:, 0:MEM] for t in h]
        for c in range(NCH):
            k.stt("dve", mh[c], memf[c], gprm[:, 16 + c:17 + c], rstd_t[:, 0:MEM], ALU.mult, ALU.mult)
        for l in range(L):
            kvb = None
            def kcb(c, ps, l=l):
                k.copy(evac_eng(), xstb[:, c * MEM:(c + 1) * MEM], ps[:, 0:MEM])
            for g in range(0, D, 256):
                wt = wload(w_xkv_d[l], 0, 16, g, 256)
                for m in range(2):
                    ps = pmain.next()
                    for kk in range(NCH):
                        k.mm(ps[:, 0:MEM], wt[:, kk, m * 128:(m + 1) * 128], mh[kk], start=(kk == 0), stop=(kk == NCH - 1))
                    kcb(g // 128 + m, ps)
            k.dma("sp", dr(kvs_d[l][:, 0:4096], kvs_tok[l]), xstb)
            for g in range(0, D, 256):
                wt = wload(w_xkv_d[l], 0, 16, D + g, 256)
                for mt in range(2):
                    ps = pmain.next()
                    for kk in range(NCH):
                        k.mm(ps[:, 0:256], mh[kk][:, mt * 128:(mt + 1) * 128], wt[:, kk, :], start=(kk == 0), stop=(kk == NCH - 1))
                    k.copy(evac_eng(), xstb[:, mt * 2048 + g: mt * 2048 + g + 256], ps[:, 0:256])
            k.dma("sp", dr(kvs_d[l][:, 4096:8192], kvs_tok[l]), xstb)

    xstb = TV(xst.ap.bitcast(BF16), xst.tok)

    def run_il(gens):
        act_ = list(gens)
        while act_:
            for g_ in list(act_):
                try:
                    next(g_)
                except StopIteration:
                    act_.remove(g_)

    def mixer(l, first_blk):
        P = prm[l]
        rmsnorm_fm(xfm, P[:, P_MIXG:P_MIXG + 16], h)
        Wl = w_in_d[l]
        st8 = {}

        def ba_gen():
            slot = wring.next()
            wba = TV(slot.ap[:, 0:16 * 16].rearrange("p (c n) -> p c n", n=16), slot.tok)
            k.dma("sp", wba, Wl[:, 5120:5136].re("(c p) n -> p c n", p=128))
            psb = pso_full[1][:, 0:TB]
            for kk in range(NCH):
                k.mm(psb[0:16, :], wba[:, kk, :], h[kk], start=(kk == 0), stop=(kk == NCH - 1))
            ba = ba_r.next()
            k.copy("dve", ba, psb[0:16, :])
            yield
            sg_ = ba_r.next()
            k.act(sg_, ba, AF.Exp, scale=-1.0)
            k.ts("dve", sg_, sg_, 1.0, None, ALU.add)
            k.recip(pack[0:16, :], sg_)
            xb = ba_r.next()
            k.ts("dve", xb, ba, P[0:16, P_DTB:P_DTB + 1], None, ALU.add)
            yield
            t1 = ba_r.next()
            k.act(t1, xb, AF.Abs)
            yield
            k.act(t1, t1, AF.Exp, scale=-1.0)
            yield
            k.act(t1, t1, AF.Ln, bias=1.0)
            k.ts("dve", xb, xb, 0.0, None, ALU.max)
            yield
            k.tt("dve", xb, xb, t1, ALU.add)
            yield
            ga = ba_r.next()
            k.ts("dve", ga, xb, nea[l][:, 0:1], None, ALU.mult)
            yield
            gb = ba_r.next()
            cur, nxt = ga, gb
            sh = 1
            while sh < 64:
                cv = cur.re("p (c j) -> p c j", j=64)
                nv = nxt.re("p (c j) -> p c j", j=64)
                k.copy("dve", nv[:, :, 0:sh], cv[:, :, 0:sh])
                k.tt("dve", nv[:, :, sh:64], cv[:, :, sh:64], cv[:, :, 0:64 - sh], ALU.add)
                cur, nxt = nxt, cur
                sh *= 2
                yield
            gc = cur
            k.copy("dve", pack[32:48, :], gc)
            gcv = gc.re("p (c j) -> p c j", j=64)
            dtmp = nxt
            dv = dtmp.re("p (c j) -> p c j", j=64)
            last = TV(gcv.ap[:, :, 63:64].to_broadcast([16, NCK, 64]), gc.tok)
            k.tt("dve", dv, last, gcv, ALU.subtract)
            yield
            k.act(pack[64:80, :], dtmp, AF.Exp)
            egc = ba_r.next()
            k.act(egc, gc, AF.Exp)
            yield
            for t in range(NT):
                ps = q4(pso_full[1], t)
                k.transpose(ps[:, 0:96], pack[0:96, t * 128:(t + 1) * 128], ident[0:96, 0:96])
            yield
            for t in range(NT):
                ps = q4(pso_full[1], t)
                k.copy("dve", colT[t], ps[:, 0:96])
                k.ts("dve", ngc[t], colT[t][:, 40:48], -1.0, None, ALU.mult)
            st8["gc"] = gc
            st8["egc"] = egc

        def conv_silu(c, ps, dst, R):
            st = R["stage"].next()
            k.copy("dve", st[:, 0:3], ctail[l][c])
            k.copy("act", st[:, 3:3 + TB], ps)
            yield
            k.copy("dve", ctail[l][c], st[:, TB:TB + 3])
            acc = R["tmpf"].next()
            a = acc[:, 0:TB]
            w = lambda j: P[:, P_DNW + j * 24 + c: P_DNW + j * 24 + c + 1]
            k.ts("dve", a, st[:, 3:3 + TB], w(3), None, ALU.mult)
            yield
            k.stt("dve", a, st[:, 2:2 + TB], w(2), a, ALU.mult, ALU.add)
            yield
            k.stt("dve", a, st[:, 1:1 + TB], w(1), a, ALU.mult, ALU.add)
            yield
            k.stt("dve", a, st[:, 0:TB], w(0), a, ALU.mult, ALU.add)
            yield
            k.act(dst, a, AF.Silu)
            yield

        def l2n(src, lnbias, out, R):
            sq = R["sq"]
            k.act(sq, src, AF.Square)
            yield
            ps = R["bank"][:, 0:TB]
            k.mm(ps, ones, sq)
            yield
            rn = R["tmpf"].next()[:, 0:TB]
            k.act(rn, ps, AF.Ln, bias=EPS)
            yield
            k.act(rn, rn, AF.Exp, scale=-0.5, bias=lnbias)
            yield
            out.append(rn)

        qh = AR[0:8]
        kh = AR[8:16]

        ssq16 = pso_full[0][0:16, 0:TB]

        def q_cb(c, ps, R):
            yield from conv_silu(c - 8, ps, qh[c - 8], R)

        def k_cb(c, ps, R):
            yield from conv_silu(c - 8, ps, kh[c - 16], R)

        def v_cb(c, ps, R):
            yield from conv_silu(c - 8, ps, vh[c - 24], R)

        def qk_gen():
            for j in range(16):
                src_ = qh[j] if j < 8 else kh[j - 8]
                sq = sqr.next()
                k.act(sq, src_, AF.Square)
                yield
                k.mm(ssq16, onesel[:, j, :], sq, start=(j == 0), stop=(j == 15))
                yield
            rn16 = ba_r.next()
            k.act(rn16, ssq16, AF.Ln, bias=EPS)
            yield
            k.act(rn16, rn16, AF.Exp, scale=-0.5, bias=cst[0:16, 444:445])
            yield
            bcb = pso_full[0][:, 0:TB]
            trb = TV(pso_full[1].ap.bitcast(BF16), pso_full[1].tok)
            for j in range(16):
                dstl = qh[j] if j < 8 else kh[j - 8]
                k.mm(bcb, sel[:, j * 128:(j + 1) * 128], rn16)
                yield
                k.tt("dve", dstl, dstl, bcb, ALU.mult)
                yield
                if j >= 8:
                    hh = j - 8
                    for t in range(NT):
                        k.transpose(trb[:, t * 128:(t + 1) * 128], dstl[:, t * 128:(t + 1) * 128], identb)
                    yield
                    for t in range(NT):
                        k.ts("dve", kdtok[t][:, hh * 128:(hh + 1) * 128], trb[:, t * 128:(t + 1) * 128],
                             colT[t][:, 72 + hh:73 + hh], None, ALU.mult)
                    yield

        def v_gen():
            for hh in range(8):
                bank = psm.next()
                trv = TV(bank.ap.bitcast(BF16), bank.tok)
                for t in range(NT):
                    k.transpose(trv[:, t * 128:(t + 1) * 128], vh[hh][:, t * 128:(t + 1) * 128], identb)
                yield
                eng = evac_eng()
                for t in range(NT):
                    k.copy(eng, vtok[t][:, hh * 128:(hh + 1) * 128], trv[:, t * 128:(t + 1) * 128])
                yield

        def z_cb(c, ps, R):
            k.act(sz[c - 32], ps, AF.Silu)
            yield

        mixed = AR[16:22] + [sz_extra[0], sz_extra[1]]

        def u_cb(c, ps, R):
            gi = c // 2
            wdw = 2 ** (gi + 1)
            st = R["stage"].next()
            k.copy("dve", st[:, 0:15], ptail[l][c])
            k.copy("act", st[:, 15:15 + TB], ps)
            yield
            k.copy("dve", ptail[l][c], st[:, TB:TB + 15])
            cur = st
            sh = 1
            while sh < wdw:
                nx = R["tmpf"].next()
                lo = 2 * sh - 1
                k.tt("dve", nx[:, lo:15 + TB], cur[:, lo:15 + TB], cur[:, lo - sh:15 + TB - sh], ALU.add)
                cur = nx
                sh *= 2
                yield
            k.stt("dve", mixed[c], cur[:, 15:15 + TB], 1.0 / wdw, st[:, 15:15 + TB], ALU.mult, ALU.subtract)
            yield
            if first_blk:
                tf = R["tmpf"].next()
                k.tt("dve", tf[:, 0:15], cur[:, 15:30], icnt[:, gi * 15:(gi + 1) * 15], ALU.mult)
                yield
                k.tt("dve", mixed[c][:, 0:15], tf[:, 0:15], st[:, 15:30], ALU.subtract)
                yield

        slots = [None] * NSLOT
        slot_i = [0]
        extras = []

        def step_all(n):
            for _ in range(n):
                for i_ in range(NSLOT):
                    g_ = slots[i_]
                    if g_ is not None:
                        try:
                            next(g_)
                        except StopIteration:
                            slots[i_] = None
                for g_ in list(extras):
                    try:
                        next(g_)
                        next(g_)
                    except StopIteration:
                        extras.remove(g_)

        def drain_slot(i_):
            while slots[i_] is not None:
                try:
                    next(slots[i_])
                except StopIteration:
                    slots[i_] = None

        def linear_pipe(n0, ncols, cbgen, nstep=9):
            for g in range(0, ncols, 256):
                wt = wload(Wl, 0, 16, n0 + g, 256)
                for m in range(2):
                    si = slot_i[0] % NSLOT
                    slot_i[0] += 1
                    drain_slot(si)
                    ps = pmain.next()
                    for kk in range(NCH):
                        k.mm(ps, wt[:, kk, m * 128:(m + 1) * 128], h[kk], start=(kk == 0), stop=(kk == NCH - 1))
                    slots[si] = cbgen((n0 + g) // 128 + m, ps, SL[si])
                step_all(nstep)

        def drain_all():
            for i_ in range(NSLOT):
                drain_slot(i_)
            while extras:
                step_all(1)

        extras.append(ba_gen())
        linear_pipe(1024, 1024, q_cb)
        drain_all()
        gc, egc = st8["gc"], st8["egc"]
        linear_pipe(2048, 1024, k_cb)
        drain_all()
        extras.append(qk_gen())
        linear_pipe(3072, 1024, v_cb)
        for i_ in range(NSLOT):
            drain_slot(i_)
        extras.append(v_gen())
        linear_pipe(4096, 1024, z_cb)
        linear_pipe(0, 1024, u_cb)
        drain_all()
        slot = wring.next()
        wp = TV(slot.ap[:, 0:8 * 256].rearrange("p (c n) -> p c n", n=256), slot.tok)
        k.dma("sp", wp, w_pool_d[l].re("(c p) n -> p c n", p=128))
        cat = h
        for gi in range(4):
            for m in range(2):
                ps = pmain.next()
                for kk in range(2):
                    k.mm(ps, wp[:, gi * 2 + kk, m * 128:(m + 1) * 128], mixed[gi * 2 + kk], start=(kk == 0), stop=(kk == 1))
                c = gi * 2 + m
                k.act(cat[c], ps, AF.Identity, scale=P[:, P_PS + c:P_PS + c + 1])

        for hh in range(8):
            k.copy("act", Sbf[hh], S[l][hh])

        def gdn_pre(t, hh, R):
            cols = slice(t * 128, (t + 1) * 128)
            bka = R["ps"].next()
            bkb = R["ps"].next()
            gB, eB, bB, kk_ps, qk_ps = q4(bka, 0), q4(bka, 1), q4(bka, 2), q4(bka, 3), q4(bkb, 0)
            k.mm(gB, sel[:, (8 + hh) * 128:(9 + hh) * 128], gc[:, cols])
            k.mm(eB, sel[:, (8 + hh) * 128:(9 + hh) * 128], egc[:, cols])
            k.mm(bB, sel[:, hh * 128:(hh + 1) * 128], pack[0:16, cols])
            k.mm(kk_ps, kh[hh][:, cols], kh[hh][:, cols])
            k.mm(qk_ps, kh[hh][:, cols], qh[hh][:, cols])
            yield
            gcol = colT[t][:, 40 + hh:41 + hh]
            ngcol = ngc[t][:, hh:hh + 1]
            bcol = colT[t][:, hh:hh + 1]
            dn = R["sm"].next()
            k.stt("dve", dn, gB, -1.0, Mnat, ALU.mult, ALU.add)
            yield
            k.act(dn, dn, AF.Exp, bias=gcol)
            dT = R["sm"].next()
            k.tt("dve", dT, gB, MT, ALU.add)
            yield
            k.act(dT, dT, AF.Exp, bias=ngcol)
            k.copy("dve", decs[t][hh], TV(eB.ap[:, 63:128:64], eB.tok))
            yield
            k.tt("dve", kgt[t][hh], kh[hh][:, cols], eB, ALU.mult)
            yield
            k.tt("dve", qdt[t][hh], qh[hh][:, cols], eB, ALU.mult)
            yield
            A = R["sm"].next()
            k.stt("dve", A, kk_ps, bcol, dn, ALU.mult, ALU.mult)
            yield
            AT = R["sm"].next()
            k.tt("dve", AT, kk_ps, dT, ALU.mult)
            yield
            k.tt("dve", AT, AT, bB, ALU.mult)
            yield
            at_ = R["sb"].next()
            a1 = R["sm"].next()
            k.tt("dve", a1, dT, ident, ALU.add)
            yield
            k.tt("dve", at_, qk_ps, a1, ALU.mult)
            yield
            T = R["sm"].next()
            k.tt("dve", T, ident, A, ALU.subtract)
            yield
            for it in range(5):
                bk = R["ps"].next()
                pT, pN = q4(bk, 0), q4(bk, 1)
                k.mm(pT, A, AT)
                if it < 4:
                    k.mm(pN, AT, A)
                yield
                A2T = R["sm"].next()
                k.copy("act", A2T, pT)
                if it < 4:
                    A2 = R["sm"].next()
                    k.copy("act", A2, pN)
                yield
                pp = q4(R["ps"].next(), 0)
                k.mm(pp, A2T, T)
                yield
                Tn = R["sm"].next()
                k.tt("dve", Tn, pp, T, ALU.add)
                T = Tn
                if it < 4:
                    A, AT = A2, A2T
                yield
            Tb = R["sb"].next()
            k.copy("dve", Tb, T)
            yield
            bk = R["ps"].next()
            pp1, pp2 = q4(bk, 0), q4(bk, 1)
            k.mm(pp1, Tb, at_)
            k.mm(pp2, Tb, kdtok[t][:, hh * 128:(hh + 1) * 128])
            yield
            k.act(p1t[t][hh], pp1, AF.Identity, scale=bcol)
            k.act(p2t[t][hh], pp2, AF.Identity, scale=bcol)
            yield

        def thread(items, R):
            for (t, hh) in items:
                yield from gdn_pre(t, hh, R)

        items = [(t, hh) for t in range(NT) for hh in range(8)]
        run_il([thread(items[i::3], GR[i]) for i in range(3)])

        for t in range(NT):
            cols = slice(t * 128, (t + 1) * 128)
            o_ps = pso
            for ci in range(2):
                rows = slice(ci * 64, (ci + 1) * 64)
                for hh in range(8):
                    bk = scan_ps.next()
                    vS, dS = q4(bk, 0), q4(bk, 1)
                    k.mm(vS[rows, :], kgt[t][hh][:, rows], Sbf[hh], tp=(0, ci * 64))
                    r0 = r0ring.next()
                    k.tt("dve", r0[rows, :], vtok[t][rows, hh * 128:(hh + 1) * 128], vS[rows, :], ALU.subtract)
                    k.mm(o_ps[hh][rows, :], qdt[t][hh][:, rows], Sbf[hh], start=True, stop=False, tp=(0, ci * 64))
                    k.mm(o_ps[hh][rows, :], p1t[t][hh][rows, rows], r0[rows, :], start=False, stop=True, tp=(ci * 64, ci * 64))
                    k.mm(dS, p2t[t][hh][rows, :], r0[rows, :], tp=(ci * 64, 0))
                    k.stt("dve", S[l][hh], S[l][hh], decs[t][hh][:, ci:ci + 1], dS, ALU.mult, ALU.add)
                    k.copy("act", Sbf[hh], S[l][hh])
            ssqs = [tiny.next() for _ in range(8)]
            ons = [GR[hh % 3]["sm"].next() for hh in range(8)]
            for hh in range(8):
                k.memset("dve", ssqs[hh], 0.0)
            for hh in range(8):
                k.act(ons[hh], o_ps[hh], AF.Square, accum=ssqs[hh][:, 0:1])
            for hh in range(8):
                k.act(ssqs[hh][:, 1:2], ssqs[hh][:, 0:1], AF.Ln, bias=EPS, scale=1.0 / 128)
            for hh in range(8):
                k.act(ssqs[hh][:, 1:2], ssqs[hh][:, 1:2], AF.Exp, scale=-0.5)
            for hh in range(8):
                k.act(ons[hh], o_ps[hh], AF.Identity, scale=ssqs[hh][:, 1:2])
            for hh in range(8):
                pt = q4(scan_ps.next(), 0)
                k.transpose(pt, ons[hh], ident)
                k.stt("dve", cat[8 + hh][:, cols], pt, P[:, P_DNG:P_DNG + 1], sz[hh][:, cols], ALU.mult, ALU.mult)
        linear_fm(w_mo_d[l], cat, 0, D, residual_cb)

    sz_extra = k.tile("szx", [128, 2, TB], BF16, n=2)

    def xattn(l):
        P = prm[l]
        rmsnorm_fm(xfm, P[:, P_XAG:P_XAG + 16], h)
        qT = AR[0:16]

        def q_cb(c, ps):
            k.copy(evac_eng(), qT[c], ps)

        linear_fm(w_xq_d[l], h, 0, D, q_cb)
        s1 = wring.next()
        s2 = wring.next()
        KT = TV(s1.ap.rearrange("p (c m) -> p c m", m=MEM), s1.tok)
        V = TV(s2.ap.rearrange("p (t d) -> p t d", d=D), s2.tok)
        k.dma("sp", s1, dr(kvs_d[l][:, 0:4096], kvs_tok[l]))
        k.dma("sp", s2, dr(kvs_d[l][:, 4096:8192], kvs_tok[l]))
        oT = h
        for a in range(4):
            for mt in range(2):
                ps = pmain.next()
                for dc in range(4):
                    k.mm(ps, KT[:, a * 4 + dc, mt * 128:(mt + 1) * 128], qT[a * 4 + dc], start=(dc == 0), stop=(dc == 3))
                k.act(Ebuf[mt], ps, AF.Exp, scale=float(512.0 ** -0.5))
            den = pmain.next()
            k.mm(den, ones, Ebuf[0], start=True, stop=False)
            k.mm(den, ones, Ebuf[1], start=False, stop=True)
            k.recip(rstd_t, den)
            for dc in range(4):
                ps = pmain.next()
                for mt in range(2):
                    k.mm(ps, V[:, mt, (a * 4 + dc) * 128:(a * 4 + dc + 1) * 128], Ebuf[mt], start=(mt == 0), stop=(mt == 1))
                k.tt("dve", oT[a * 4 + dc], ps, rstd_t, ALU.mult)
        linear_fm(w_xo_d[l], oT, 0, D, residual_cb)

    def ffn(l):
        P = prm[l]
        rmsnorm_fm(xfm, P[:, P_FFG:P_FFG + 16], h)
        hid = AR
        for half in range(2):
            for g in range(11):
                n0 = half * 2816 + g * 256
                wg = wload(w_gate_d[l], 0, 16, n0, 256)
                wu = wload(w_up_d[l], 0, 16, n0, 256)
                for m in range(2):
                    c = n0 // 128 + m
                    cl = g * 2 + m
                    gps = pmain.next()
                    for kk in range(NCH):
                        k.mm(gps, wg[:, kk, m * 128:(m + 1) * 128], h[kk], start=(kk == 0), stop=(kk == NCH - 1))
                    ups = pmain.next()
                    for kk in range(NCH):
                        k.mm(ups, wu[:, kk, m * 128:(m + 1) * 128], h[kk], start=(kk == 0), stop=(kk == NCH - 1))
                    st = SL[0]["stage"].next()
                    k.copy("dve", st[:, 0:2], ftail[l][c])
                    k.copy("act", st[:, 2:2 + TB], gps)
                    k.copy("dve", ftail[l][c], st[:, TB:TB + 2])
                    acc = SL[0]["tmpf"].next()[:, 0:TB]
                    w = lambda j: P[:, P_FCW + j * 44 + c: P_FCW + j * 44 + c + 1]
                    k.ts("dve", acc, st[:, 2:2 + TB], w(2), P[:, P_FCB + c:P_FCB + c + 1], ALU.mult, ALU.add)
                    k.stt("dve", acc, st[:, 1:1 + TB], w(1), acc, ALU.mult, ALU.add)
                    k.stt("dve", acc, st[:, 0:TB], w(0), acc, ALU.mult, ALU.add)
                    k.act(acc, acc, AF.Silu)
                    k.tt("dve", hid[cl], ups, acc, ALU.mult)
            linear_fm(w_down_d[l][half * 2816:(half + 1) * 2816, :], hid, 0, D, residual_cb)

    for s in range(NSEQ):
        if "mem" in STAGES:
            mem_setup(s)
        for l in range(L):
            for tl in S[l] + ptail[l] + ctail[l] + ftail[l]:
                k.memset("dve", tl, 0.0)
        for b in range(NBLK):
            t0 = (s * NBLK + b) * TB

            def xcb(c, tt_, ps, eng):
                k.copy(eng, xfm[c][:, tt_ * 128:(tt_ + 1) * 128], ps)

            load_tokmajor_to_fm(x_d[t0:t0 + TB, :], NT, xcb)
            for l in range(L):
                if "mixer" in STAGES:
                    mixer(l, b == 0)
                if dbg and s == 0 and b == 0 and l == 0:
                    dump(0)
                if "xattn" in STAGES:
                    xattn(l)
                if dbg and s == 0 and b == 0 and l == 0:
                    dump(1)
                if "ffn" in STAGES:
                    ffn(l)
                if dbg and s == 0 and b == 0 and l == 0:
                    dump(2)
            if final:
                ps = pmain.next()
                for c in range(NCH):
                    sq = sqr.next()
                    k.act(sq, xfm[c], AF.Square)
                    k.mm(ps, ones, sq, start=(c == 0), stop=(c == NCH - 1))
                k.act(rstd_t, ps, AF.Ln, bias=EPS, scale=1.0 / D)
                k.act(rstd_t, rstd_t, AF.Exp, scale=-0.5)
                for c in range(NCH):
                    k.stt("dve", xfm[c], xfm[c], gprm[:, c:c + 1], rstd_t, ALU.mult, ALU.mult)
            for tt_ in range(NT):
                for c in range(NCH):
                    pt = q4(psm.next(), 0)
                    k.transpose(pt, xfm[c][:, tt_ * 128:(tt_ + 1) * 128], ident)
                    k.copy(evac_eng(), xst[:, c * 128:(c + 1) * 128], pt)
                k.dma("sp", dr(out_d[t0 + tt_ * 128:t0 + (tt_ + 1) * 128, :], out_tok), xst)
    k.finalize()
    return nc, k


def make_consts():
    cst = np.zeros((128, 448), np.float32)
    cst[:, 0:128] = np.eye(128, dtype=np.float32)
    p = np.arange(128)[:, None]
    f = np.arange(128)[None, :]
    same = (p // 64) == (f // 64)
    cst[:, 128:256] = np.where(same & (p > f), 0.0, NEG)
    cst[:, 256:384] = np.where(same & (f > p), 0.0, NEG)
    for gi in range(4):
        w = 2 ** (gi + 1)
        cnt = np.minimum(np.arange(1, 16), w).astype(np.float32)
        cst[:, 384 + gi * 15:384 + (gi + 1) * 15] = (1.0 / cnt)[None, :]
    cst[0:8, 444] = np.log(128.0 ** -0.5)
    sel = np.zeros((16, 16, 128), np.float32)
    for j in range(16):
        sel[j, j, :] = 1.0
    return cst, sel.reshape(16, 16 * 128)


def colmajor(v):
    return np.ascontiguousarray(v.reshape(-1, 128).T)


def pack_params(inp, l):
    P = np.zeros((128, NCOL), np.float32)
    P[:, P_MIXG:P_MIXG + 16] = colmajor(inp["mix_norm_g"][l])
    P[:, P_XAG:P_XAG + 16] = colmajor(inp["xa_norm_g"][l])
    P[:, P_FFG:P_FFG + 16] = colmajor(inp["ffn_norm_g"][l])
    P[:, P_PS:P_PS + 8] = colmajor(inp["pool_scale"][l])
    for j in range(4):
        P[:, P_DNW + j * 24:P_DNW + (j + 1) * 24] = colmajor(inp["dn_conv_w"][l][j])
    P[:, P_DNG] = inp["dn_norm_g"][l]
    for j in range(3):
        P[:, P_FCW + j * 44:P_FCW + (j + 1) * 44] = colmajor(inp["ffn_conv_w"][l][j])
    P[:, P_FCB:P_FCB + 44] = colmajor(inp["ffn_conv_b"][l])
    P[8:16, P_DTB] = inp["dn_dt_bias"][l]
    P[8:16, P_ALOG] = inp["dn_a_log"][l]
    return P


_CACHE = {}


def get_program(NSEQ, NBLK, TB, L, final, dbg=False):
    key = (NSEQ, NBLK, TB, L, final, dbg)
    if key not in _CACHE:
        _CACHE[key] = build_program(NSEQ, NBLK, TB, L, final, dbg)[0]
    return _CACHE[key]


TB_DEFAULT = 256
KVS_KIND = "Internal"
NPSM = 3
SILU_ENG = "pool"
TAP_ENG = "pool"
STAGES = {"mem", "mixer", "xattn", "ffn"}
WNAMES = ["w_in", "w_mix_out", "w_xq", "w_xkv", "w_xo", "w_gate", "w_up", "w_down"]


def kernel(**inp):
    inp = {k_: np.asarray(v) for k_, v in inp.items()}
    n = 8
    B = inp["x"].shape[0]
    NSEQ = B // n
    L = inp["w_in"].shape[0]
    TB = TB_DEFAULT
    nc = get_program(NSEQ, SEQ // TB, TB, L, True)
    cst, sel = make_consts()
    prm = np.stack([pack_params(inp, l) for l in range(L)])
    gprm = np.concatenate([colmajor(inp["final_norm_g"]), colmajor(inp["mem_norm_g"])], axis=1)
    shared = {w: np.ascontiguousarray(inp[w], dtype=np.float32) for w in WNAMES}
    shared["w_pool"] = np.ascontiguousarray(inp["w_pool"].reshape(L, 1024, 256))
    shared.update(prm=prm, gprm=np.ascontiguousarray(gprm), cst=cst, sel=sel)
    in_maps = []
    for c in range(n):
        m = dict(shared)
        m["x"] = np.ascontiguousarray(inp["x"][c * NSEQ:(c + 1) * NSEQ].reshape(NSEQ * SEQ, D))
        m["mem"] = np.ascontiguousarray(inp["mem"][c * NSEQ:(c + 1) * NSEQ].reshape(NSEQ * MEM, D))
        in_maps.append(m)
    res = run_bass_kernel_spmd(nc, in_maps, core_ids=list(range(n)))
    out = np.concatenate([r["out"].reshape(NSEQ, SEQ, D) for r in res.results], axis=0)
    return out.astype(np.float32)
```

```python
import contextlib
import numpy as np
import concourse.bass as bass
import concourse.mybir as mybir
from concourse.bass_utils import run_bass_kernel_spmd

F32 = mybir.dt.float32
BF16 = mybir.dt.bfloat16
AF = mybir.ActivationFunctionType
ALU = mybir.AluOpType

SEM_LIMIT = 30000
DMA_POOL = 8

D = 2048
NCH = 16
INW = 5136
DFF = 5632
MEM = 256
SEQ = 2048
NEG = -30000.0
EPS = 1e-6

P_MIXG, P_XAG, P_FFG, P_PS, P_DNW, P_DNG, P_FCW, P_FCB, P_DTB, P_ALOG, NCOL = 0, 16, 32, 48, 56, 152, 153, 285, 329, 330, 332


class Tok:
    __slots__ = ("name", "lw", "rd")

    def __init__(self, name=""):
        self.name = name
        self.lw = None
        self.rd = []


class TV:
    __slots__ = ("ap", "tok")

    def __init__(self, ap, tok):
        self.ap = ap
        self.tok = tok

    def __getitem__(self, idx):
        return TV(self.ap[idx], self.tok)

    def re(self, s, **kw):
        return TV(self.ap.rearrange(s, **kw), self.tok)


class Op:
    __slots__ = ("eng", "fn", "deps", "sig", "sem", "val", "idx", "isdma", "prev")

    def __init__(self, eng, fn, isdma):
        self.eng = eng
        self.fn = fn
        self.deps = []
        self.sig = False
        self.sem = None
        self.val = 0
        self.isdma = isdma
        self.prev = None


class Ring:
    def __init__(self, items):
        self.items = items
        self.i = 0

    def next(self):
        it = self.items[self.i % len(self.items)]
        self.i += 1
        return it


class K:
    ENGS = ("pe", "act", "dve", "pool", "sp")

    def __init__(self, nc):
        self.nc = nc
        self.ops = {e: [] for e in self.ENGS}
        self.es = contextlib.ExitStack()
        self.nops = 0

    def tile(self, name, shape, dt, n=None, psum=False):
        cm = self.nc.psum_tensor(name, shape, dt) if psum else self.nc.sbuf_tensor(name, shape, dt)
        t = self.es.enter_context(cm)
        if n is None:
            return TV(t[:], Tok(name))
        return [TV(t[:, i], Tok(f"{name}{i}")) for i in range(n)]

    def add(self, eng, fn, reads, writes, isdma=False):
        op = Op(eng, fn, isdma)
        op.idx = self.nops
        self.nops += 1
        deps = set()
        rtoks = [r.tok for r in reads if isinstance(r, TV)]
        for t in rtoks:
            if t.lw is not None:
                deps.add(t.lw)
        for w in writes:
            t = w.tok
            if t.lw is not None:
                deps.add(t.lw)
            for rr in t.rd:
                deps.add(rr)
        for d in deps:
            if d is op:
                continue
            if d.eng == eng and not d.isdma and not isdma:
                if eng == "pe":
                    continue
                if not any(t.lw is d for t in rtoks) and not any(w.tok.lw is d for w in writes):
                    continue
            op.deps.append(d)
            d.sig = True
        for t in rtoks:
            t.rd.append(op)
        for w in writes:
            w.tok.lw = op
            w.tok.rd = []
        self.ops[eng].append(op)
        return op

    @staticmethod
    def _a(x):
        return x.ap if isinstance(x, TV) else x

    def mm(self, out, lhsT, rhs, start=True, stop=True, tp=None):
        o, l, r = out.ap, lhsT.ap, rhs.ap
        if tp is None:
            f = lambda e: e.matmul(o, lhsT=l, rhs=r, start=start, stop=stop)
        else:
            f = lambda e: e.matmul(o, lhsT=l, rhs=r, start=start, stop=stop, tile_position=tp)
        return self.add("pe", f, [lhsT, rhs], [out])

    def transpose(self, out, in_, ident):
        o, i, d = out.ap, in_.ap, ident.ap
        return self.add("pe", lambda e: e.transpose(o, i, d), [in_, ident], [out])

    def act(self, out, in_, func, bias=0.0, scale=1.0, accum=None):
        o, i = out.ap, in_.ap
        b, s = self._a(bias), self._a(scale)
        if accum is not None:
            ac = accum.ap
            f = lambda e: e.activation(o, i, func, bias=b, scale=s, accum_out=ac)
            wr = [out, accum]
        else:
            f = lambda e: e.activation(o, i, func, bias=b, scale=s)
            wr = [out]
        return self.add("act", f, [in_, bias, scale], wr)

    def tt(self, eng, out, in0, in1, op):
        o, a, b = out.ap, in0.ap, in1.ap
        return self.add(eng, lambda e: e.tensor_tensor(o, a, b, op), [in0, in1], [out])

    def ts(self, eng, out, in0, s1, s2, op0, op1=None):
        o, a = out.ap, in0.ap
        x1, x2 = self._a(s1), self._a(s2)
        if op1 is None:
            f = lambda e: e.tensor_single_scalar(o, a, x1, op0)
        else:
            f = lambda e: e.tensor_scalar(o, a, x1, x2, op0, op1)
        return self.add(eng, f, [in0, s1, s2], [out])

    def stt(self, eng, out, in0, scalar, in1, op0, op1):
        o, a, b = out.ap, in0.ap, in1.ap
        sc = self._a(scalar)
        return self.add(eng, lambda e: e.scalar_tensor_tensor(o, a, sc, b, op0, op1),
                        [in0, scalar, in1], [out])

    def copy(self, eng, out, in_):
        o, i = out.ap, in_.ap
        if eng == "act":
            return self.add(eng, lambda e: e.copy(o, i), [in_], [out])
        return self.add(eng, lambda e: e.tensor_copy(o, i), [in_], [out])

    def memset(self, eng, out, val):
        o = out.ap
        return self.add(eng, lambda e: e.memset(o, val), [], [out])

    def recip(self, out, in_):
        o, i = out.ap, in_.ap
        return self.add("dve", lambda e: e.reciprocal(o, i), [in_], [out])

    def dma(self, q, out, in_):
        o, i = out.ap, in_.ap
        return self.add(q, lambda e: e.dma_start(out=o, in_=i), [in_], [out], isdma=True)

    def finalize(self):
        nc, es = self.nc, self.es
        engobj = {"pe": "tensor", "act": "scalar", "dve": "vector", "pool": "gpsimd", "sp": "sync"}
        for e in self.ENGS:
            sems, cnt = [], 0
            dma_sems, dma_cnt, ndma = [], [], 0
            for op in self.ops[e]:
                if op.isdma:
                    if len(dma_sems) < DMA_POOL:
                        dma_sems.append(es.enter_context(nc.semaphore(f"d_{e}{len(dma_sems)}")))
                        dma_cnt.append(0)
                    j = ndma % DMA_POOL
                    ndma += 1
                    op.prev = (dma_sems[j], dma_cnt[j]) if dma_cnt[j] > 0 else None
                    dma_cnt[j] += 16
                    op.sem, op.val, op.sig = dma_sems[j], dma_cnt[j], True
                elif op.sig:
                    if not sems or cnt >= SEM_LIMIT:
                        sems.append(es.enter_context(nc.semaphore(f"c_{e}{len(sems)}")))
                        cnt = 0
                    cnt += 1
                    op.sem, op.val = sems[-1], cnt
        block = es.enter_context(nc.Block())
        for e in self.ENGS:
            ops = self.ops[e]
            if not ops:
                continue

            def emit(eng, ops=ops):
                seen = {}
                for op in ops:
                    waits = {}
                    cands = [(d.sem, d.val) for d in op.deps]
                    if op.isdma and op.prev is not None:
                        cands.append(op.prev)
                    for sem, val in cands:
                        key = id(sem)
                        if seen.get(key, 0) >= val:
                            continue
                        if key not in waits or waits[key][1] < val:
                            waits[key] = (sem, val)
                    for key, (sem, val) in waits.items():
                        eng.wait_ge(sem, val)
                        seen[key] = val
                    ins = op.fn(eng)
                    if op.isdma:
                        ins.then_inc(op.sem, 16)
                    elif op.sig:
                        ins.then_inc(op.sem, 1)
                last = {}
                for op in ops:
                    if op.isdma:
                        last[id(op.sem)] = (op.sem, op.val)
                for key, (sem, val) in last.items():
                    if seen.get(key, 0) < val:
                        eng.wait_ge(sem, val)

            getattr(block, engobj[e])(emit)
        es.close()


def build_program(NSEQ, NBLK, TB, L, final, dbg=False):
    nc = bass.Bass("TRN2", target_bir_lowering=False)
    NT = TB // 128
    NCK = TB // 64
    NTOK = NSEQ * NBLK * TB

    def din(name, shape, dt=F32):
        return nc.dram_tensor(name, shape, dt, kind="ExternalInput").ap()

    x_d = din("x", [NTOK, D])
    mem_d = din("mem", [NSEQ * MEM, D])
    WSHAPES = [("w_xkv", D, 2 * D), ("w_in", D, INW), ("w_pool", 1024, 256), ("w_mix_out", D, D), ("w_xq", D, D),
               ("w_xo", D, D), ("w_gate", D, DFF), ("w_up", D, DFF), ("w_down", DFF, D)]
    wf32, wb16 = {}, {}
    for nm, K_, N_ in WSHAPES:
        wf32[nm] = din(nm, [L, K_, N_])
        wb16[nm] = nc.dram_tensor(nm + "_b", [L, K_, N_], BF16, kind="Internal").ap()
    prm_d = din("prm", [L, 128, NCOL])
    gprm_d = din("gprm", [128, 32])
    cst_d = din("cst", [128, 448])
    sel_d = din("sel", [16, 16 * 128])
    out_d = nc.dram_tensor("out", [NTOK, D], F32, kind="ExternalOutput").ap()
    kvs_d = nc.dram_tensor("kvs", [L, 128, 8192], BF16, kind=KVS_KIND).ap() if "mem" in STAGES else None
    dbg_d = None
    if dbg:
        dbg_d = nc.dram_tensor("dbg", [4, 128, NCH * TB], F32, kind="ExternalOutput").ap()

    k = K(nc)
    wtok = Tok("w")

    def dr(ap, tok=None):
        return TV(ap, tok if tok is not None else wtok)

    kvs_tok = [Tok(f"kvs{l}") for l in range(L)]
    WT = {nm: [TV(wb16[nm][l], Tok(f"{nm}{l}")) for l in range(L)] for nm, _, _ in WSHAPES}

    def convert(nm, l, K_):
        for r in range(0, K_, 128):
            k.dma("pool", WT[nm][l][r:r + 128, :], dr(wf32[nm][l][r:r + 128, :]))

    for l in range(L):
        convert("w_xkv", l, D)
    for l in range(L):
        for nm, K_, N_ in WSHAPES[1:]:
            convert(nm, l, K_)
    w_in_d, w_pool_d, w_mo_d, w_xq_d = WT["w_in"], WT["w_pool"], WT["w_mix_out"], WT["w_xq"]
    w_xkv_d, w_xo_d, w_gate_d, w_up_d, w_down_d = WT["w_xkv"], WT["w_xo"], WT["w_gate"], WT["w_up"], WT["w_down"]
    out_tok = Tok("out")

    cst = k.tile("cst_s", [128, 448], F32)
    k.dma("sp", cst, dr(cst_d))
    ident, Mnat, MT, icnt = cst[:, 0:128], cst[:, 128:256], cst[:, 256:384], cst[:, 384:444]
    sel = k.tile("sel_s", [16, 16 * 128], F32)
    k.dma("sp", sel, dr(sel_d))
    ones = k.tile("ones", [128, 128], BF16)
    k.memset("dve", ones, 1.0)
    onesel = k.tile("onesel", [128, 17, 16], BF16)
    k.memset("dve", onesel, 0.0)
    for j_ in range(16):
        k.memset("dve", onesel[:, j_, j_:j_ + 1], 1.0)
    identb = k.tile("identb", [128, 128], BF16)
    k.copy("dve", identb, ident)
    gprm = k.tile("gprm_s", [128, 32], F32)
    k.dma("sp", gprm, dr(gprm_d))
    prm = []
    nea = []
    for l in range(L):
        p = k.tile(f"prm_s{l}", [128, NCOL], F32)
        k.dma("sp", p, dr(prm_d[l]))
        prm.append(p)
        ne = k.tile(f"nea{l}", [16, 1], F32)
        k.act(ne, p[0:16, P_ALOG:P_ALOG + 1], AF.Exp)
        k.ts("dve", ne, ne, -1.0, None, ALU.mult)
        nea.append(ne)

    S = [k.tile(f"S{l}", [128, 8, 128], F32, n=8) for l in range(L)]
    Sbf = k.tile("Sbf", [128, 8, 128], BF16, n=8)
    ptail = [k.tile(f"ptail{l}", [128, 8, 15], F32, n=8) for l in range(L)]
    ctail = [k.tile(f"ctail{l}", [128, 24, 3], F32, n=24) for l in range(L)]
    ftail = [k.tile(f"ftail{l}", [128, 44, 2], F32, n=44) for l in range(L)]

    xfm = k.tile("xfm", [128, NCH, TB], F32, n=NCH)
    h = k.tile("h", [128, NCH, TB], BF16, n=NCH)
    AR = k.tile("ar", [128, 22, TB], BF16, n=22)
    sz = k.tile("sz", [128, 8, TB], BF16, n=8)
    vh = k.tile("vh", [128, 8, TB], BF16, n=8)
    vtok = k.tile("vtok", [128, NT, 1024], BF16, n=NT)
    kdtok = k.tile("kdtok", [128, NT, 1024], BF16, n=NT)
    xst = k.tile("xst", [128, D], F32)
    wring = Ring(k.tile("wring", [128, 5, 4096], BF16, n=5))
    _pmain_full = k.tile("pmain", [128, 3, 512], F32, n=3, psum=True)
    pmain = Ring([t[:, 0:TB] for t in _pmain_full])
    _psm_full = k.tile("psm", [128, NPSM, 512], F32, n=NPSM, psum=True)
    psm = Ring(_psm_full)
    _pso = k.tile("pso", [128, 2, 512], F32, n=2, psum=True)
    pso = [_pso[i // 4][:, (i % 4) * 128:(i % 4 + 1) * 128] for i in range(8)]
    pso_full = _pso
    _six = list(_psm_full) + list(_pmain_full)
    _eight = _six + list(_pso)
    GR = []
    for i_ in range(4):
        GR.append({"ps": Ring([_eight[2 * i_], _eight[2 * i_ + 1]]),
                   "sm": Ring(k.tile(f"gsm{i_}", [128, 9, 128], F32, n=9)),
                   "sb": Ring(k.tile(f"gsb{i_}", [128, 4, 128], BF16, n=4))})
    r0ring = Ring(k.tile("r0r", [128, 4, 128], BF16, n=4))
    scan_ps = Ring(_six)
    NSLOT = 3
    _slot_banks = list(_psm_full)
    SL = []
    for i_ in range(NSLOT):
        SL.append({"tmpf": Ring(k.tile(f"sl_t{i_}", [128, 3, TB + 16], F32, n=3)),
                   "stage": Ring(k.tile(f"sl_s{i_}", [128, 2, TB + 16], F32, n=2)),
                   "sq": k.tile(f"sl_q{i_}", [128, TB], BF16),
                   "bank": _slot_banks[i_]})

    def q4(bank, j):
        return bank[:, j * 128:(j + 1) * 128]

    sqr = Ring(k.tile("sqr", [128, 2, TB], BF16, n=2))
    rstd_t = k.tile("rstd_t", [128, TB], F32)
    p1t = [k.tile(f"p1t{t_}", [128, 8, 128], BF16, n=8) for t_ in range(NT)]
    p2t = [k.tile(f"p2t{t_}", [128, 8, 128], BF16, n=8) for t_ in range(NT)]
    kgt = [k.tile(f"kgt{t_}", [128, 8, 128], BF16, n=8) for t_ in range(NT)]
    qdt = [k.tile(f"qdt{t_}", [128, 8, 128], BF16, n=8) for t_ in range(NT)]
    decs = [k.tile(f"decs{t_}", [128, 8, 2], F32, n=8) for t_ in range(NT)]
    ba_r = Ring(k.tile("ba_r", [16, 7, TB], F32, n=7))
    pack = k.tile("pack", [128, TB], F32)
    k.memset("dve", pack, 0.0)
    colT = k.tile("colT", [128, NT, 96], F32, n=NT)
    ngc = k.tile("ngc", [128, NT, 8], F32, n=NT)
    ssq8 = k.tile("ssq8", [128, 8], F32)
    rs8 = k.tile("rs8", [128, 8], F32)
    Ebuf = k.tile("Ebuf", [128, 2, TB], BF16, n=2)

    evac_i = [0]

    def evac_eng():
        evac_i[0] += 1
        return "act" if evac_i[0] % 2 == 0 else "dve"

    def wload(W2d, k0, kc, n0, ncols):
        slot = wring.next()
        view = TV(slot.ap[:, 0:kc * ncols].rearrange("p (c n) -> p c n", n=ncols), slot.tok)
        src = W2d[k0 * 128:(k0 + kc) * 128, n0:n0 + ncols].re("(c p) n -> p c n", p=128)
        k.dma("sp", view, src)
        return view

    def linear_fm(W2d, kch, n0, ncols, cb, colw=256):
        nk = len(kch)
        parts = []
        s = 0
        while s < nk:
            e = min(nk, s + 16) if nk <= 16 or nk > 22 else min(nk, s + 11)
            parts.append((s, e))
            s = e
        for g in range(0, ncols, colw):
            w = min(colw, ncols - g)
            tiles = [wload(W2d, s, e - s, n0 + g, w) for (s, e) in parts]
            for m in range(w // 128):
                ps = pmain.next()
                for pi, (s, e) in enumerate(parts):
                    for kk in range(s, e):
                        k.mm(ps, tiles[pi][:, kk - s, m * 128:(m + 1) * 128], kch[kk],
                             start=(kk == 0), stop=(kk == nk - 1))
                cb((n0 + g) // 128 + m, ps)

    def rmsnorm_fm(src, gcol, dst, dst_dt_scale=1.0):
        ps = pmain.next()
        for c in range(NCH):
            sq = sqr.next()
            k.act(sq, src[c], AF.Square)
            k.mm(ps, ones, sq, start=(c == 0), stop=(c == NCH - 1))
        k.act(rstd_t, ps, AF.Ln, bias=EPS, scale=1.0 / D)
        k.act(rstd_t, rstd_t, AF.Exp, scale=-0.5)
        for c in range(NCH):
            k.stt("dve", dst[c], src[c], gcol[:, c:c + 1], rstd_t, ALU.mult, ALU.mult)

    def residual_cb(c, ps):
        k.tt("dve", xfm[c], xfm[c], ps, ALU.add)

    def dump(i):
        if dbg_d is not None:
            for c in range(NCH):
                k.dma("sp", dr(dbg_d[i][:, c * TB:(c + 1) * TB], out_tok), xfm[c])

    def load_tokmajor_to_fm(src2d, ntok_tiles, dst_cb):
        for tt_ in range(ntok_tiles):
            k.dma("sp", xst, dr(src2d[tt_ * 128:(tt_ + 1) * 128, :]))
            for c4 in range(4):
                bank = psm.next()
                for j in range(4):
                    c = c4 * 4 + j
                    k.transpose(q4(bank, j), xst[:, c * 128:(c + 1) * 128], ident)
                eng = evac_eng()
                for j in range(4):
                    dst_cb(c4 * 4 + j, tt_, q4(bank, j), eng)

    def mem_setup(s):
        memf = [TV(t.ap[:, 0:MEM] if TB >= MEM else None, t.tok) for t in xfm]
        assert TB >= MEM

        def cb(c, tt_, ps, eng):
            k.copy(eng, memf[c][:, tt_ * 128:(tt_ + 1) * 128], ps)

        load_tokmajor_to_fm(mem_d[s * MEM:(s + 1) * MEM, :], MEM // 128, cb)
        ps = pmain.next()
        for c in range(NCH):
            sq = sqr.next()
            k.act(sq[:, 0:MEM], memf[c], AF.Square)
            k.mm(ps[:, 0:MEM], ones, sq[:, 0:MEM], start=(c == 0), stop=(c == NCH - 1))
        k.act(rstd_t[:, 0:MEM], ps[:, 0:MEM], AF.Ln, bias=EPS, scale=1.0 / D)
        k.act(rstd_t[:, 0:MEM], rstd_t[:, 0:MEM], AF.Exp, scale=-0.5)
        mh = [t[:, 0:MEM] for t in h]
        for c in range(NCH):
            k.stt("dve", mh[c], memf[c], gprm[:, 16 + c:17 + c], rstd_t[:, 0:MEM], ALU.mult, ALU.mult)
        for l in range(L):
            kvb = None
            def kcb(c, ps, l=l):
                k.copy(evac_eng(), xstb[:, c * MEM:(c + 1) * MEM], ps[:, 0:MEM])
            for g in range(0, D, 256):
                wt = wload(w_xkv_d[l], 0, 16, g, 256)
                for m in range(2):
                    ps = pmain.next()
                    for kk in range(NCH):
                        k.mm(ps[:, 0:MEM], wt[:, kk, m * 128:(m + 1) * 128], mh[kk], start=(kk == 0), stop=(kk == NCH - 1))
                    kcb(g // 128 + m, ps)
            k.dma("sp", dr(kvs_d[l][:, 0:4096], kvs_tok[l]), xstb)
            for g in range(0, D, 256):
                wt = wload(w_xkv_d[l], 0, 16, D + g, 256)
                for mt in range(2):
                    ps = pmain.next()
                    for kk in range(NCH):
                        k.mm(ps[:, 0:256], mh[kk][:, mt * 128:(mt + 1) * 128], wt[:, kk, :], start=(kk == 0), stop=(kk == NCH - 1))
                    k.copy(evac_eng(), xstb[:, mt * 2048 + g: mt * 2048 + g + 256], ps[:, 0:256])
            k.dma("sp", dr(kvs_d[l][:, 4096:8192], kvs_tok[l]), xstb)

    xstb = TV(xst.ap.bitcast(BF16), xst.tok)

    def run_il(gens):
        act_ = list(gens)
        while act_:
            for g_ in list(act_):
                try:
                    next(g_)
                except StopIteration:
                    act_.remove(g_)

    def mixer(l, first_blk):
        P = prm[l]
        rmsnorm_fm(xfm, P[:, P_MIXG:P_MIXG + 16], h)
        Wl = w_in_d[l]
        st8 = {}

        def ba_gen():
            slot = wring.next()
            wba = TV(slot.ap[:, 0:16 * 16].rearrange("p (c n) -> p c n", n=16), slot.tok)
            k.dma("sp", wba, Wl[:, 5120:5136].re("(c p) n -> p c n", p=128))
            psb = pso_full[1][:, 0:TB]
            for kk in range(NCH):
                k.mm(psb[0:16, :], wba[:, kk, :], h[kk], start=(kk == 0), stop=(kk == NCH - 1))
            ba = ba_r.next()
            k.copy("dve", ba, psb[0:16, :])
            yield
            sg_ = ba_r.next()
            k.act(sg_, ba, AF.Exp, scale=-1.0)
            k.ts("dve", sg_, sg_, 1.0, None, ALU.add)
            k.recip(pack[0:16, :], sg_)
            xb = ba_r.next()
            k.ts("dve", xb, ba, P[0:16, P_DTB:P_DTB + 1], None, ALU.add)
            yield
            t1 = ba_r.next()
            k.act(t1, xb, AF.Abs)
            yield
            k.act(t1, t1, AF.Exp, scale=-1.0)
            yield
            k.act(t1, t1, AF.Ln, bias=1.0)
            k.ts("dve", xb, xb, 0.0, None, ALU.max)
            yield
            k.tt("dve", xb, xb, t1, ALU.add)
            yield
            ga = ba_r.next()
            k.ts("dve", ga, xb, nea[l][:, 0:1], None, ALU.mult)
            yield
            gb = ba_r.next()
            cur, nxt = ga, gb
            sh = 1
            while sh < 64:
                cv = cur.re("p (c j) -> p c j", j=64)
                nv = nxt.re("p (c j) -> p c j", j=64)
                k.copy("dve", nv[:, :, 0:sh], cv[:, :, 0:sh])
                k.tt("dve", nv[:, :, sh:64], cv[:, :, sh:64], cv[:, :, 0:64 - sh], ALU.add)
                cur, nxt = nxt, cur
                sh *= 2
                yield
            gc = cur
            k.copy("dve", pack[32:48, :], gc)
            gcv = gc.re("p (c j) -> p c j", j=64)
            dtmp = nxt
            dv = dtmp.re("p (c j) -> p c j", j=64)
            last = TV(gcv.ap[:, :, 63:64].to_broadcast([16, NCK, 64]), gc.tok)
            k.tt("dve", dv, last, gcv, ALU.subtract)
            yield
            k.act(pack[64:80, :], dtmp, AF.Exp)
            egc = ba_r.next()
            k.act(egc, gc, AF.Exp)
            yield
            for t in range(NT):
                ps = q4(pso_full[1], t)
                k.transpose(ps[:, 0:96], pack[0:96, t * 128:(t + 1) * 128], ident[0:96, 0:96])
            yield
            for t in range(NT):
                ps = q4(pso_full[1], t)
                k.copy("dve", colT[t], ps[:, 0:96])
                k.ts("dve", ngc[t], colT[t][:, 40:48], -1.0, None, ALU.mult)
            st8["gc"] = gc
            st8["egc"] = egc

        def conv_silu(c, ps, dst, R):
            st = R["stage"].next()
            k.copy("dve", st[:, 0:3], ctail[l][c])
            k.copy("act", st[:, 3:3 + TB], ps)
            yield
            k.copy("dve", ctail[l][c], st[:, TB:TB + 3])
            acc = R["tmpf"].next()
            a = acc[:, 0:TB]
            w = lambda j: P[:, P_DNW + j * 24 + c: P_DNW + j * 24 + c + 1]
            k.act(a, ps, AF.Identity, scale=w(3))
            yield
            k.stt("dve", a, st[:, 2:2 + TB], w(2), a, ALU.mult, ALU.add)
            yield
            k.stt("dve", a, st[:, 1:1 + TB], w(1), a, ALU.mult, ALU.add)
            yield
            k.stt("dve", a, st[:, 0:TB], w(0), a, ALU.mult, ALU.add)
            yield
            k.act(dst, a, AF.Silu)
            yield

        def l2n(src, lnbias, out, R):
            sq = R["sq"]
            k.act(sq, src, AF.Square)
            yield
            ps = R["bank"][:, 0:TB]
            k.mm(ps, ones, sq)
            yield
            rn = R["tmpf"].next()[:, 0:TB]
            k.act(rn, ps, AF.Ln, bias=EPS)
            yield
            k.act(rn, rn, AF.Exp, scale=-0.5, bias=lnbias)
            yield
            out.append(rn)

        qh = AR[0:8]
        kh = AR[8:16]

        ssq16 = pso_full[0][0:16, 0:TB]

        def q_cb(c, ps, R):
            yield from conv_silu(c - 8, ps, qh[c - 8], R)

        def k_cb(c, ps, R):
            yield from conv_silu(c - 8, ps, kh[c - 16], R)

        def v_cb(c, ps, R):
            yield from conv_silu(c - 8, ps, vh[c - 24], R)

        def qk_gen():
            for j in range(16):
                src_ = qh[j] if j < 8 else kh[j - 8]
                sq = sqr.next()
                k.act(sq, src_, AF.Square)
                yield
                k.mm(ssq16, onesel[:, j, :], sq, start=(j == 0), stop=(j == 15))
                yield
            rn16 = ba_r.next()
            k.act(rn16, ssq16, AF.Ln, bias=EPS)
            yield
            k.act(rn16, rn16, AF.Exp, scale=-0.5, bias=cst[0:16, 444:445])
            yield
            bcb = pso_full[0][:, 0:TB]
            trb = TV(pso_full[1].ap.bitcast(BF16), pso_full[1].tok)
            for j in range(16):
                dstl = qh[j] if j < 8 else kh[j - 8]
                k.mm(bcb, sel[:, j * 128:(j + 1) * 128], rn16)
                yield
                k.tt("dve", dstl, dstl, bcb, ALU.mult)
                yield
                if j >= 8:
                    hh = j - 8
                    for t in range(NT):
                        k.transpose(trb[:, t * 128:(t + 1) * 128], dstl[:, t * 128:(t + 1) * 128], identb)
                    yield
                    for t in range(NT):
                        k.ts("dve", kdtok[t][:, hh * 128:(hh + 1) * 128], trb[:, t * 128:(t + 1) * 128],
                             colT[t][:, 72 + hh:73 + hh], None, ALU.mult)
                    yield

        def v_gen():
            for hh in range(8):
                bank = psm.next()
                trv = TV(bank.ap.bitcast(BF16), bank.tok)
                for t in range(NT):
                    k.transpose(trv[:, t * 128:(t + 1) * 128], vh[hh][:, t * 128:(t + 1) * 128], identb)
                yield
                eng = evac_eng()
                for t in range(NT):
                    k.copy(eng, vtok[t][:, hh * 128:(hh + 1) * 128], trv[:, t * 128:(t + 1) * 128])
                yield

        def z_cb(c, ps, R):
            k.act(sz[c - 32], ps, AF.Silu)
            yield

        mixed = AR[16:22] + [sz_extra[0], sz_extra[1]]

        def u_cb(c, ps, R):
            gi = c // 2
            wdw = 2 ** (gi + 1)
            st = R["stage"].next()
            k.copy("dve", st[:, 0:15], ptail[l][c])
            k.copy("act", st[:, 15:15 + TB], ps)
            yield
            k.copy("dve", ptail[l][c], st[:, TB:TB + 15])
            cur = st
            sh = 1
            while sh < wdw:
                nx = R["tmpf"].next()
                lo = 2 * sh - 1
                k.tt("dve", nx[:, lo:15 + TB], cur[:, lo:15 + TB], cur[:, lo - sh:15 + TB - sh], ALU.add)
                cur = nx
                sh *= 2
                yield
            k.stt("dve", mixed[c], cur[:, 15:15 + TB], 1.0 / wdw, st[:, 15:15 + TB], ALU.mult, ALU.subtract)
            yield
            if first_blk:
                tf = R["tmpf"].next()
                k.tt("dve", tf[:, 0:15], cur[:, 15:30], icnt[:, gi * 15:(gi + 1) * 15], ALU.mult)
                yield
                k.tt("dve", mixed[c][:, 0:15], tf[:, 0:15], st[:, 15:30], ALU.subtract)
                yield

        slots = [None] * NSLOT
        slot_i = [0]
        extras = []

        def step_all(n):
            for _ in range(n):
                for i_ in range(NSLOT):
                    g_ = slots[i_]
                    if g_ is not None:
                        try:
                            next(g_)
                        except StopIteration:
                            slots[i_] = None
                for g_ in list(extras):
                    try:
                        next(g_)
                        next(g_)
                    except StopIteration:
                        extras.remove(g_)

        def drain_slot(i_):
            while slots[i_] is not None:
                try:
                    next(slots[i_])
                except StopIteration:
                    slots[i_] = None

        def linear_pipe(n0, ncols, cbgen, nstep=9):
            for g in range(0, ncols, 256):
                wt = wload(Wl, 0, 16, n0 + g, 256)
                for m in range(2):
                    si = slot_i[0] % NSLOT
                    slot_i[0] += 1
                    drain_slot(si)
                    ps = pmain.next()
                    for kk in range(NCH):
                        k.mm(ps, wt[:, kk, m * 128:(m + 1) * 128], h[kk], start=(kk == 0), stop=(kk == NCH - 1))
                    slots[si] = cbgen((n0 + g) // 128 + m, ps, SL[si])
                step_all(nstep)

        def drain_all():
            for i_ in range(NSLOT):
                drain_slot(i_)
            while extras:
                step_all(1)

        extras.append(ba_gen())
        linear_pipe(1024, 1024, q_cb)
        drain_all()
        gc, egc = st8["gc"], st8["egc"]
        linear_pipe(2048, 1024, k_cb)
        drain_all()
        extras.append(qk_gen())
        linear_pipe(3072, 1024, v_cb)
        for i_ in range(NSLOT):
            drain_slot(i_)
        extras.append(v_gen())
        linear_pipe(4096, 1024, z_cb)
        linear_pipe(0, 1024, u_cb)
        drain_all()
        slot = wring.next()
        wp = TV(slot.ap[:, 0:8 * 256].rearrange("p (c n) -> p c n", n=256), slot.tok)
        k.dma("sp", wp, w_pool_d[l].re("(c p) n -> p c n", p=128))
        cat = h
        for gi in range(4):
            for m in range(2):
                ps = pmain.next()
                for kk in range(2):
                    k.mm(ps, wp[:, gi * 2 + kk, m * 128:(m + 1) * 128], mixed[gi * 2 + kk], start=(kk == 0), stop=(kk == 1))
                c = gi * 2 + m
                k.act(cat[c], ps, AF.Identity, scale=P[:, P_PS + c:P_PS + c + 1])

        for hh in range(8):
            k.copy("act", Sbf[hh], S[l][hh])

        def gdn_pre(t, hh, R):
            cols = slice(t * 128, (t + 1) * 128)
            bka = R["ps"].next()
            bkb = R["ps"].next()
            gB, eB, bB, kk_ps, qk_ps = q4(bka, 0), q4(bka, 1), q4(bka, 2), q4(bka, 3), q4(bkb, 0)
            k.mm(gB, sel[:, (8 + hh) * 128:(9 + hh) * 128], gc[:, cols])
            k.mm(eB, sel[:, (8 + hh) * 128:(9 + hh) * 128], egc[:, cols])
            k.mm(bB, sel[:, hh * 128:(hh + 1) * 128], pack[0:16, cols])
            k.mm(kk_ps, kh[hh][:, cols], kh[hh][:, cols])
            k.mm(qk_ps, kh[hh][:, cols], qh[hh][:, cols])
            yield
            gcol = colT[t][:, 40 + hh:41 + hh]
            ngcol = ngc[t][:, hh:hh + 1]
            bcol = colT[t][:, hh:hh + 1]
            dn = R["sm"].next()
            k.stt("dve", dn, gB, -1.0, Mnat, ALU.mult, ALU.add)
            yield
            k.act(dn, dn, AF.Exp, bias=gcol)
            dT = R["sm"].next()
            k.tt("dve", dT, gB, MT, ALU.add)
            yield
            k.act(dT, dT, AF.Exp, bias=ngcol)
            k.copy("dve", decs[t][hh], TV(eB.ap[:, 63:128:64], eB.tok))
            yield
            k.tt("dve", kgt[t][hh], kh[hh][:, cols], eB, ALU.mult)
            yield
            k.tt("dve", qdt[t][hh], qh[hh][:, cols], eB, ALU.mult)
            yield
            A = R["sm"].next()
            k.stt("dve", A, kk_ps, bcol, dn, ALU.mult, ALU.mult)
            yield
            AT = R["sm"].next()
            k.tt("dve", AT, kk_ps, dT, ALU.mult)
            yield
            k.tt("dve", AT, AT, bB, ALU.mult)
            yield
            at_ = R["sb"].next()
            a1 = R["sm"].next()
            k.tt("dve", a1, dT, ident, ALU.add)
            yield
            k.tt("dve", at_, qk_ps, a1, ALU.mult)
            yield
            T = R["sm"].next()
            k.tt("dve", T, ident, A, ALU.subtract)
            yield
            for it in range(5):
                bk = R["ps"].next()
                pT, pN = q4(bk, 0), q4(bk, 1)
                k.mm(pT, A, AT)
                if it < 4:
                    k.mm(pN, AT, A)
                yield
                A2T = R["sm"].next()
                k.copy("act", A2T, pT)
                if it < 4:
                    A2 = R["sm"].next()
                    k.copy("act", A2, pN)
                yield
                pp = q4(R["ps"].next(), 0)
                k.mm(pp, A2T, T)
                yield
                Tn = R["sm"].next()
                k.tt("dve", Tn, pp, T, ALU.add)
                T = Tn
                if it < 4:
                    A, AT = A2, A2T
                yield
            Tb = R["sb"].next()
            k.copy("dve", Tb, T)
            yield
            bk = R["ps"].next()
            pp1, pp2 = q4(bk, 0), q4(bk, 1)
            k.mm(pp1, Tb, at_)
            k.mm(pp2, Tb, kdtok[t][:, hh * 128:(hh + 1) * 128])
            yield
            k.act(p1t[t][hh], pp1, AF.Identity, scale=bcol)
            k.act(p2t[t][hh], pp2, AF.Identity, scale=bcol)
            yield

        def thread(items, R):
            for (t, hh) in items:
                yield from gdn_pre(t, hh, R)

        items = [(t, hh) for t in range(NT) for hh in range(8)]
        run_il([thread(items[i::4], GR[i]) for i in range(4)])

        for t in range(NT):
            cols = slice(t * 128, (t + 1) * 128)
            o_ps = pso
            for ci in range(2):
                rows = slice(ci * 64, (ci + 1) * 64)
                for hh in range(8):
                    bk = scan_ps.next()
                    vS, dS = q4(bk, 0), q4(bk, 1)
                    k.mm(vS[rows, :], kgt[t][hh][:, rows], Sbf[hh], tp=(0, ci * 64))
                    r0 = r0ring.next()
                    k.tt("dve", r0[rows, :], vtok[t][rows, hh * 128:(hh + 1) * 128], vS[rows, :], ALU.subtract)
                    k.mm(o_ps[hh][rows, :], qdt[t][hh][:, rows], Sbf[hh], start=True, stop=False, tp=(0, ci * 64))
                    k.mm(o_ps[hh][rows, :], p1t[t][hh][rows, rows], r0[rows, :], start=False, stop=True, tp=(ci * 64, ci * 64))
                    k.mm(dS, p2t[t][hh][rows, :], r0[rows, :], tp=(ci * 64, 0))
                    k.stt("dve", S[l][hh], S[l][hh], decs[t][hh][:, ci:ci + 1], dS, ALU.mult, ALU.add)
                    k.copy("act", Sbf[hh], S[l][hh])
            ons = [GR[hh % 4]["sm"].next() for hh in range(8)]
            k.memset("dve", ssq8, 0.0)
            for hh in range(8):
                k.act(ons[hh], o_ps[hh], AF.Square, accum=ssq8[:, hh:hh + 1])
            k.act(rs8, ssq8, AF.Ln, bias=EPS, scale=1.0 / 128)
            k.act(rs8, rs8, AF.Exp, scale=-0.5)
            for hh in range(8):
                k.act(ons[hh], o_ps[hh], AF.Identity, scale=rs8[:, hh:hh + 1])
            for hh in range(8):
                pt = q4(scan_ps.next(), 0)
                k.transpose(pt, ons[hh], ident)
                k.stt("dve", cat[8 + hh][:, cols], pt, P[:, P_DNG:P_DNG + 1], sz[hh][:, cols], ALU.mult, ALU.mult)
        linear_fm(w_mo_d[l], cat, 0, D, residual_cb)

    sz_extra = k.tile("szx", [128, 2, TB], BF16, n=2)

    def xattn(l):
        P = prm[l]
        rmsnorm_fm(xfm, P[:, P_XAG:P_XAG + 16], h)
        qT = AR[0:16]

        def q_cb(c, ps):
            k.copy(evac_eng(), qT[c], ps)

        linear_fm(w_xq_d[l], h, 0, D, q_cb)
        s1 = wring.next()
        s2 = wring.next()
        KT = TV(s1.ap.rearrange("p (c m) -> p c m", m=MEM), s1.tok)
        V = TV(s2.ap.rearrange("p (t d) -> p t d", d=D), s2.tok)
        k.dma("sp", s1, dr(kvs_d[l][:, 0:4096], kvs_tok[l]))
        k.dma("sp", s2, dr(kvs_d[l][:, 4096:8192], kvs_tok[l]))
        oT = h
        for a in range(4):
            for mt in range(2):
                ps = pmain.next()
                for dc in range(4):
                    k.mm(ps, KT[:, a * 4 + dc, mt * 128:(mt + 1) * 128], qT[a * 4 + dc], start=(dc == 0), stop=(dc == 3))
                k.act(Ebuf[mt], ps, AF.Exp, scale=float(512.0 ** -0.5))
            den = pmain.next()
            k.mm(den, ones, Ebuf[0], start=True, stop=False)
            k.mm(den, ones, Ebuf[1], start=False, stop=True)
            k.recip(rstd_t, den)
            for dc in range(4):
                ps = pmain.next()
                for mt in range(2):
                    k.mm(ps, V[:, mt, (a * 4 + dc) * 128:(a * 4 + dc + 1) * 128], Ebuf[mt], start=(mt == 0), stop=(mt == 1))
                k.tt("dve", oT[a * 4 + dc], ps, rstd_t, ALU.mult)
        linear_fm(w_xo_d[l], oT, 0, D, residual_cb)

    def ffn(l):
        P = prm[l]
        rmsnorm_fm(xfm, P[:, P_FFG:P_FFG + 16], h)
        hid = AR
        for half in range(2):
            for g in range(11):
                n0 = half * 2816 + g * 256
                wg = wload(w_gate_d[l], 0, 16, n0, 256)
                wu = wload(w_up_d[l], 0, 16, n0, 256)
                for m in range(2):
                    c = n0 // 128 + m
                    cl = g * 2 + m
                    gps = pmain.next()
                    for kk in range(NCH):
                        k.mm(gps, wg[:, kk, m * 128:(m + 1) * 128], h[kk], start=(kk == 0), stop=(kk == NCH - 1))
                    ups = pmain.next()
                    for kk in range(NCH):
                        k.mm(ups, wu[:, kk, m * 128:(m + 1) * 128], h[kk], start=(kk == 0), stop=(kk == NCH - 1))
                    st = SL[0]["stage"].next()
                    k.copy("dve", st[:, 0:2], ftail[l][c])
                    k.copy("act", st[:, 2:2 + TB], gps)
                    k.copy("dve", ftail[l][c], st[:, TB:TB + 2])
                    acc = SL[0]["tmpf"].next()[:, 0:TB]
                    w = lambda j: P[:, P_FCW + j * 44 + c: P_FCW + j * 44 + c + 1]
                    k.act(acc, gps, AF.Identity, scale=w(2), bias=P[:, P_FCB + c:P_FCB + c + 1])
                    k.stt("dve", acc, st[:, 1:1 + TB], w(1), acc, ALU.mult, ALU.add)
                    k.stt("dve", acc, st[:, 0:TB], w(0), acc, ALU.mult, ALU.add)
                    k.act(acc, acc, AF.Silu)
                    k.tt("dve", hid[cl], ups, acc, ALU.mult)
            linear_fm(w_down_d[l][half * 2816:(half + 1) * 2816, :], hid, 0, D, residual_cb)

    for s in range(NSEQ):
        if "mem" in STAGES:
            mem_setup(s)
        for l in range(L):
            for tl in S[l] + ptail[l] + ctail[l] + ftail[l]:
                k.memset("dve", tl, 0.0)
        for b in range(NBLK):
            t0 = (s * NBLK + b) * TB

            def xcb(c, tt_, ps, eng):
                k.copy(eng, xfm[c][:, tt_ * 128:(tt_ + 1) * 128], ps)

            load_tokmajor_to_fm(x_d[t0:t0 + TB, :], NT, xcb)
            for l in range(L):
                if "mixer" in STAGES:
                    mixer(l, b == 0)
                if dbg and s == 0 and b == 0 and l == 0:
                    dump(0)
                if "xattn" in STAGES:
                    xattn(l)
                if dbg and s == 0 and b == 0 and l == 0:
                    dump(1)
                if "ffn" in STAGES:
                    ffn(l)
                if dbg and s == 0 and b == 0 and l == 0:
                    dump(2)
            if final:
                ps = pmain.next()
                for c in range(NCH):
                    sq = sqr.next()
                    k.act(sq, xfm[c], AF.Square)
                    k.mm(ps, ones, sq, start=(c == 0), stop=(c == NCH - 1))
                k.act(rstd_t, ps, AF.Ln, bias=EPS, scale=1.0 / D)
                k.act(rstd_t, rstd_t, AF.Exp, scale=-0.5)
                for c in range(NCH):
                    k.stt("dve", xfm[c], xfm[c], gprm[:, c:c + 1], rstd_t, ALU.mult, ALU.mult)
            for tt_ in range(NT):
                for c in range(NCH):
                    pt = q4(psm.next(), 0)
                    k.transpose(pt, xfm[c][:, tt_ * 128:(tt_ + 1) * 128], ident)
                    k.copy(evac_eng(), xst[:, c * 128:(c + 1) * 128], pt)
                k.dma("sp", dr(out_d[t0 + tt_ * 128:t0 + (tt_ + 1) * 128, :], out_tok), xst)
    k.finalize()
    return nc, k


def make_consts():
    cst = np.zeros((128, 448), np.float32)
    cst[:, 0:128] = np.eye(128, dtype=np.float32)
    p = np.arange(128)[:, None]
    f = np.arange(128)[None, :]
    same = (p // 64) == (f // 64)
    cst[:, 128:256] = np.where(same & (p > f), 0.0, NEG)
    cst[:, 256:384] = np.where(same & (f > p), 0.0, NEG)
    for gi in range(4):
        w = 2 ** (gi + 1)
        cnt = np.minimum(np.arange(1, 16), w).astype(np.float32)
        cst[:, 384 + gi * 15:384 + (gi + 1) * 15] = (1.0 / cnt)[None, :]
    cst[0:8, 444] = np.log(128.0 ** -0.5)
    sel = np.zeros((16, 16, 128), np.float32)
    for j in range(16):
        sel[j, j, :] = 1.0
    return cst, sel.reshape(16, 16 * 128)


def colmajor(v):
    return np.ascontiguousarray(v.reshape(-1, 128).T)


def pack_params(inp, l):
    P = np.zeros((128, NCOL), np.float32)
    P[:, P_MIXG:P_MIXG + 16] = colmajor(inp["mix_norm_g"][l])
    P[:, P_XAG:P_XAG + 16] = colmajor(inp["xa_norm_g"][l])
    P[:, P_FFG:P_FFG + 16] = colmajor(inp["ffn_norm_g"][l])
    P[:, P_PS:P_PS + 8] = colmajor(inp["pool_scale"][l])
    for j in range(4):
        P[:, P_DNW + j * 24:P_DNW + (j + 1) * 24] = colmajor(inp["dn_conv_w"][l][j])
    P[:, P_DNG] = inp["dn_norm_g"][l]
    for j in range(3):
        P[:, P_FCW + j * 44:P_FCW + (j + 1) * 44] = colmajor(inp["ffn_conv_w"][l][j])
    P[:, P_FCB:P_FCB + 44] = colmajor(inp["ffn_conv_b"][l])
    P[8:16, P_DTB] = inp["dn_dt_bias"][l]
    P[8:16, P_ALOG] = inp["dn_a_log"][l]
    return P


_CACHE = {}


def get_program(NSEQ, NBLK, TB, L, final, dbg=False):
    key = (NSEQ, NBLK, TB, L, final, dbg)
    if key not in _CACHE:
        _CACHE[key] = build_program(NSEQ, NBLK, TB, L, final, dbg)[0]
    return _CACHE[key]


TB_DEFAULT = 256
KVS_KIND = "Internal"
NPSM = 3
SILU_ENG = "pool"
TAP_ENG = "pool"
STAGES = {"mem", "mixer", "xattn", "ffn"}
WNAMES = ["w_in", "w_mix_out", "w_xq", "w_xkv", "w_xo", "w_gate", "w_up", "w_down"]


def kernel(**inp):
    inp = {k_: np.asarray(v) for k_, v in inp.items()}
    n = 8
    B = inp["x"].shape[0]
    NSEQ = B // n
    L = inp["w_in"].shape[0]
    TB = TB_DEFAULT
    nc = get_program(NSEQ, SEQ // TB, TB, L, True)
    cst, sel = make_consts()
    prm = np.stack([pack_params(inp, l) for l in range(L)])
    gprm = np.concatenate([colmajor(inp["final_norm_g"]), colmajor(inp["mem_norm_g"])], axis=1)
    shared = {w: np.ascontiguousarray(inp[w], dtype=np.float32) for w in WNAMES}
    shared["w_pool"] = np.ascontiguousarray(inp["w_pool"].reshape(L, 1024, 256))
    shared.update(prm=prm, gprm=np.ascontiguousarray(gprm), cst=cst, sel=sel)
    in_maps = []
    for c in range(n):
        m = dict(shared)
        m["x"] = np.ascontiguousarray(inp["x"][c * NSEQ:(c + 1) * NSEQ].reshape(NSEQ * SEQ, D))
        m["mem"] = np.ascontiguousarray(inp["mem"][c * NSEQ:(c + 1) * NSEQ].reshape(NSEQ * MEM, D))
        in_maps.append(m)
    res = run_bass_kernel_spmd(nc, in_maps, core_ids=list(range(n)))
    out = np.concatenate([r["out"].reshape(NSEQ, SEQ, D) for r in res.results], axis=0)
    return out.astype(np.float32)
```

```python
import contextlib
import numpy as np
import concourse.bass as bass
import concourse.mybir as mybir
from concourse.bass_utils import run_bass_kernel_spmd

F32 = mybir.dt.float32
BF16 = mybir.dt.bfloat16
AF = mybir.ActivationFunctionType
ALU = mybir.AluOpType

SEM_LIMIT = 30000
DMA_POOL = 8

D = 2048
NCH = 16
INW = 5136
DFF = 5632
MEM = 256
SEQ = 2048
NEG = -30000.0
EPS = 1e-6

P_MIXG, P_XAG, P_FFG, P_PS, P_DNW, P_DNG, P_FCW, P_FCB, P_DTB, P_ALOG, NCOL = 0, 16, 32, 48, 56, 152, 153, 285, 329, 330, 332


class Tok:
    __slots__ = ("name", "lw", "rd")

    def __init__(self, name=""):
        self.name = name
        self.lw = None
        self.rd = []


class TV:
    __slots__ = ("ap", "tok")

    def __init__(self, ap, tok):
        self.ap = ap
        self.tok = tok

    def __getitem__(self, idx):
        return TV(self.ap[idx], self.tok)

    def re(self, s, **kw):
        return TV(self.ap.rearrange(s, **kw), self.tok)


class Op:
    __slots__ = ("eng", "fn", "deps", "sig", "sem", "val", "idx", "isdma", "prev")

    def __init__(self, eng, fn, isdma):
        self.eng = eng
        self.fn = fn
        self.deps = []
        self.sig = False
        self.sem = None
        self.val = 0
        self.isdma = isdma
        self.prev = None


class Ring:
    def __init__(self, items):
        self.items = items
        self.i = 0

    def next(self):
        it = self.items[self.i % len(self.items)]
        self.i += 1
        return it


class K:
    ENGS = ("pe", "act", "dve", "pool", "sp")

    def __init__(self, nc):
        self.nc = nc
        self.ops = {e: [] for e in self.ENGS}
        self.es = contextlib.ExitStack()
        self.nops = 0

    def tile(self, name, shape, dt, n=None, psum=False):
        cm = self.nc.psum_tensor(name, shape, dt) if psum else self.nc.sbuf_tensor(name, shape, dt)
        t = self.es.enter_context(cm)
        if n is None:
            return TV(t[:], Tok(name))
        return [TV(t[:, i], Tok(f"{name}{i}")) for i in range(n)]

    def add(self, eng, fn, reads, writes, isdma=False):
        op = Op(eng, fn, isdma)
        op.idx = self.nops
        self.nops += 1
        deps = set()
        rtoks = [r.tok for r in reads if isinstance(r, TV)]
        for t in rtoks:
            if t.lw is not None:
                deps.add(t.lw)
        for w in writes:
            t = w.tok
            if t.lw is not None:
                deps.add(t.lw)
            for rr in t.rd:
                deps.add(rr)
        for d in deps:
            if d is op:
                continue
            if d.eng == eng and not d.isdma and not isdma:
                if eng == "pe":
                    continue
                if not any(t.lw is d for t in rtoks) and not any(w.tok.lw is d for w in writes):
                    continue
            op.deps.append(d)
            d.sig = True
        for t in rtoks:
            t.rd.append(op)
        for w in writes:
            w.tok.lw = op
            w.tok.rd = []
        self.ops[eng].append(op)
        return op

    @staticmethod
    def _a(x):
        return x.ap if isinstance(x, TV) else x

    def mm(self, out, lhsT, rhs, start=True, stop=True, tp=None):
        o, l, r = out.ap, lhsT.ap, rhs.ap
        if tp is None:
            f = lambda e: e.matmul(o, lhsT=l, rhs=r, start=start, stop=stop)
        else:
            f = lambda e: e.matmul(o, lhsT=l, rhs=r, start=start, stop=stop, tile_position=tp)
        return self.add("pe", f, [lhsT, rhs], [out])

    def transpose(self, out, in_, ident):
        o, i, d = out.ap, in_.ap, ident.ap
        return self.add("pe", lambda e: e.transpose(o, i, d), [in_, ident], [out])

    def act(self, out, in_, func, bias=0.0, scale=1.0, accum=None):
        o, i = out.ap, in_.ap
        b, s = self._a(bias), self._a(scale)
        if accum is not None:
            ac = accum.ap
            f = lambda e: e.activation(o, i, func, bias=b, scale=s, accum_out=ac)
            wr = [out, accum]
        else:
            f = lambda e: e.activation(o, i, func, bias=b, scale=s)
            wr = [out]
        return self.add("act", f, [in_, bias, scale], wr)

    def tt(self, eng, out, in0, in1, op):
        o, a, b = out.ap, in0.ap, in1.ap
        return self.add(eng, lambda e: e.tensor_tensor(o, a, b, op), [in0, in1], [out])

    def ts(self, eng, out, in0, s1, s2, op0, op1=None):
        o, a = out.ap, in0.ap
        x1, x2 = self._a(s1), self._a(s2)
        if op1 is None:
            f = lambda e: e.tensor_single_scalar(o, a, x1, op0)
        else:
            f = lambda e: e.tensor_scalar(o, a, x1, x2, op0, op1)
        return self.add(eng, f, [in0, s1, s2], [out])

    def stt(self, eng, out, in0, scalar, in1, op0, op1):
        o, a, b = out.ap, in0.ap, in1.ap
        sc = self._a(scalar)
        return self.add(eng, lambda e: e.scalar_tensor_tensor(o, a, sc, b, op0, op1),
                        [in0, scalar, in1], [out])

    def copy(self, eng, out, in_):
        o, i = out.ap, in_.ap
        if eng == "act":
            return self.add(eng, lambda e: e.copy(o, i), [in_], [out])
        return self.add(eng, lambda e: e.tensor_copy(o, i), [in_], [out])

    def memset(self, eng, out, val):
        o = out.ap
        return self.add(eng, lambda e: e.memset(o, val), [], [out])

    def recip(self, out, in_):
        o, i = out.ap, in_.ap
        return self.add("dve", lambda e: e.reciprocal(o, i), [in_], [out])

    def dma(self, q, out, in_):
        o, i = out.ap, in_.ap
        return self.add(q, lambda e: e.dma_start(out=o, in_=i), [in_], [out], isdma=True)

    def finalize(self):
        nc, es = self.nc, self.es
        engobj = {"pe": "tensor", "act": "scalar", "dve": "vector", "pool": "gpsimd", "sp": "sync"}
        for e in self.ENGS:
            sems, cnt = [], 0
            dma_sems, dma_cnt, ndma = [], [], 0
            for op in self.ops[e]:
                if op.isdma:
                    if len(dma_sems) < DMA_POOL:
                        dma_sems.append(es.enter_context(nc.semaphore(f"d_{e}{len(dma_sems)}")))
                        dma_cnt.append(0)
                    j = ndma % DMA_POOL
                    ndma += 1
                    op.prev = (dma_sems[j], dma_cnt[j]) if dma_cnt[j] > 0 else None
                    dma_cnt[j] += 16
                    op.sem, op.val, op.sig = dma_sems[j], dma_cnt[j], True
                elif op.sig:
                    if not sems or cnt >= SEM_LIMIT:
                        sems.append(es.enter_context(nc.semaphore(f"c_{e}{len(sems)}")))
                        cnt = 0
                    cnt += 1
                    op.sem, op.val = sems[-1], cnt
        block = es.enter_context(nc.Block())
        for e in self.ENGS:
            ops = self.ops[e]
            if not ops:
                continue

            def emit(eng, ops=ops):
                seen = {}
                for op in ops:
                    waits = {}
                    cands = [(d.sem, d.val) for d in op.deps]
                    if op.isdma and op.prev is not None:
                        cands.append(op.prev)
                    for sem, val in cands:
                        key = id(sem)
                        if seen.get(key, 0) >= val:
                            continue
                        if key not in waits or waits[key][1] < val:
                            waits[key] = (sem, val)
                    for key, (sem, val) in waits.items():
                        eng.wait_ge(sem, val)
                        seen[key] = val
                    ins = op.fn(eng)
                    if op.isdma:
                        ins.then_inc(op.sem, 16)
                    elif op.sig:
                        ins.then_inc(op.sem, 1)
                last = {}
                for op in ops:
                    if op.isdma:
                        last[id(op.sem)] = (op.sem, op.val)
                for key, (sem, val) in last.items():
                    if seen.get(key, 0) < val:
                        eng.wait_ge(sem, val)

            getattr(block, engobj[e])(emit)
        es.close()


def build_program(NSEQ, NBLK, TB, L, final, dbg=False):
    nc = bass.Bass("TRN2", target_bir_lowering=False)
    NT = TB // 128
    NCK = TB // 64
    NTOK = NSEQ * NBLK * TB

    def din(name, shape, dt=F32):
        return nc.dram_tensor(name, shape, dt, kind="ExternalInput").ap()

    x_d = din("x", [NTOK, D])
    mem_d = din("mem", [NSEQ * MEM, D])
    WSHAPES = [("w_xkv", D, 2 * D), ("w_in", D, INW), ("w_pool", 1024, 256), ("w_mix_out", D, D), ("w_xq", D, D),
               ("w_xo", D, D), ("w_gate", D, DFF), ("w_up", D, DFF), ("w_down", DFF, D)]
    wf32, wb16 = {}, {}
    for nm, K_, N_ in WSHAPES:
        wf32[nm] = din(nm, [L, K_, N_])
        wb16[nm] = nc.dram_tensor(nm + "_b", [L, K_, N_], BF16, kind="Internal").ap()
    prm_d = din("prm", [L, 128, NCOL])
    gprm_d = din("gprm", [128, 32])
    cst_d = din("cst", [128, 448])
    sel_d = din("sel", [16, 16 * 128])
    out_d = nc.dram_tensor("out", [NTOK, D], F32, kind="ExternalOutput").ap()
    kvs_d = nc.dram_tensor("kvs", [L, 128, 8192], BF16, kind=KVS_KIND).ap() if "mem" in STAGES else None
    dbg_d = None
    if dbg:
        dbg_d = nc.dram_tensor("dbg", [4, 128, NCH * TB], F32, kind="ExternalOutput").ap()

    k = K(nc)
    wtok = Tok("w")

    def dr(ap, tok=None):
        return TV(ap, tok if tok is not None else wtok)

    kvs_tok = [Tok(f"kvs{l}") for l in range(L)]
    WT = {nm: [TV(wb16[nm][l], Tok(f"{nm}{l}")) for l in range(L)] for nm, _, _ in WSHAPES}

    def convert(nm, l, K_):
        for r in range(0, K_, 128):
            k.dma("pool", WT[nm][l][r:r + 128, :], dr(wf32[nm][l][r:r + 128, :]))

    for l in range(L):
        convert("w_xkv", l, D)
    for l in range(L):
        for nm, K_, N_ in WSHAPES[1:]:
            convert(nm, l, K_)
    w_in_d, w_pool_d, w_mo_d, w_xq_d = WT["w_in"], WT["w_pool"], WT["w_mix_out"], WT["w_xq"]
    w_xkv_d, w_xo_d, w_gate_d, w_up_d, w_down_d = WT["w_xkv"], WT["w_xo"], WT["w_gate"], WT["w_up"], WT["w_down"]
    out_tok = Tok("out")

    cst = k.tile("cst_s", [128, 448], F32)
    k.dma("sp", cst, dr(cst_d))
    ident, Mnat, MT, icnt = cst[:, 0:128], cst[:, 128:256], cst[:, 256:384], cst[:, 384:444]
    sel = k.tile("sel_s", [16, 16 * 128], F32)
    k.dma("sp", sel, dr(sel_d))
    ones = k.tile("ones", [128, 128], BF16)
    k.memset("dve", ones, 1.0)
    onesel = k.tile("onesel", [128, 17, 16], BF16)
    k.memset("dve", onesel, 0.0)
    for j_ in range(16):
        k.memset("dve", onesel[:, j_, j_:j_ + 1], 1.0)
    identb = k.tile("identb", [128, 128], BF16)
    k.copy("dve", identb, ident)
    gprm = k.tile("gprm_s", [128, 32], F32)
    k.dma("sp", gprm, dr(gprm_d))
    prm = []
    nea = []
    for l in range(L):
        p = k.tile(f"prm_s{l}", [128, NCOL], F32)
        k.dma("sp", p, dr(prm_d[l]))
        prm.append(p)
        ne = k.tile(f"nea{l}", [16, 1], F32)
        k.act(ne, p[0:16, P_ALOG:P_ALOG + 1], AF.Exp)
        k.ts("dve", ne, ne, -1.0, None, ALU.mult)
        nea.append(ne)

    S = [k.tile(f"S{l}", [128, 8, 128], F32, n=8) for l in range(L)]
    Sbf = k.tile("Sbf", [128, 8, 128], BF16, n=8)
    ptail = [k.tile(f"ptail{l}", [128, 8, 15], F32, n=8) for l in range(L)]
    ctail = [k.tile(f"ctail{l}", [128, 24, 3], F32, n=24) for l in range(L)]
    ftail = [k.tile(f"ftail{l}", [128, 44, 2], F32, n=44) for l in range(L)]

    xfm = k.tile("xfm", [128, NCH, TB], F32, n=NCH)
    h = k.tile("h", [128, NCH, TB], BF16, n=NCH)
    AR = k.tile("ar", [128, 22, TB], BF16, n=22)
    sz = k.tile("sz", [128, 8, TB], BF16, n=8)
    vh = k.tile("vh", [128, 8, TB], BF16, n=8)
    vtok = k.tile("vtok", [128, NT, 1024], BF16, n=NT)
    kdtok = k.tile("kdtok", [128, NT, 1024], BF16, n=NT)
    xst = k.tile("xst", [128, D], F32)
    wring = Ring(k.tile("wring", [128, 5, 4096], BF16, n=5))
    _pmain_full = k.tile("pmain", [128, 3, 512], F32, n=3, psum=True)
    pmain = Ring([t[:, 0:TB] for t in _pmain_full])
    _psm_full = k.tile("psm", [128, NPSM, 512], F32, n=NPSM, psum=True)
    psm = Ring(_psm_full)
    _pso = k.tile("pso", [128, 2, 512], F32, n=2, psum=True)
    pso = [_pso[i // 4][:, (i % 4) * 128:(i % 4 + 1) * 128] for i in range(8)]
    pso_full = _pso
    _six = list(_psm_full) + list(_pmain_full)
    _eight = _six + list(_pso)
    GR = []
    for i_ in range(4):
        GR.append({"ps": Ring([_eight[2 * i_], _eight[2 * i_ + 1]]),
                   "sm": Ring(k.tile(f"gsm{i_}", [128, 9, 128], F32, n=9)),
                   "sb": Ring(k.tile(f"gsb{i_}", [128, 4, 128], BF16, n=4))})
    r0ring = Ring(k.tile("r0r", [128, 4, 128], BF16, n=4))
    scan_ps = Ring(_six)
    NSLOT = 3
    _slot_banks = list(_psm_full)
    SL = []
    for i_ in range(NSLOT):
        SL.append({"tmpf": Ring(k.tile(f"sl_t{i_}", [128, 3, TB + 16], F32, n=3)),
                   "stage": Ring(k.tile(f"sl_s{i_}", [128, 2, TB + 16], F32, n=2)),
                   "sq": k.tile(f"sl_q{i_}", [128, TB], BF16),
                   "bank": _slot_banks[i_]})

    def q4(bank, j):
        return bank[:, j * 128:(j + 1) * 128]

    sqr = Ring(k.tile("sqr", [128, 2, TB], BF16, n=2))
    rstd_t = k.tile("rstd_t", [128, TB], F32)
    p1t = [k.tile(f"p1t{t_}", [128, 8, 128], BF16, n=8) for t_ in range(NT)]
    p2t = [k.tile(f"p2t{t_}", [128, 8, 128], BF16, n=8) for t_ in range(NT)]
    kgt = [k.tile(f"kgt{t_}", [128, 8, 128], BF16, n=8) for t_ in range(NT)]
    qdt = [k.tile(f"qdt{t_}", [128, 8, 128], BF16, n=8) for t_ in range(NT)]
    decs = [k.tile(f"decs{t_}", [128, 8, 2], F32, n=8) for t_ in range(NT)]
    ba_r = Ring(k.tile("ba_r", [16, 7, TB], F32, n=7))
    pack = k.tile("pack", [128, TB], F32)
    k.memset("dve", pack, 0.0)
    colT = k.tile("colT", [128, NT, 96], F32, n=NT)
    ngc = k.tile("ngc", [128, NT, 8], F32, n=NT)
    ssq8 = k.tile("ssq8", [128, 8], F32)
    rs8 = k.tile("rs8", [128, 8], F32)
    Ebuf = k.tile("Ebuf", [128, 2, TB], BF16, n=2)

    evac_i = [0]

    def evac_eng():
        evac_i[0] += 1
        return "act" if evac_i[0] % 2 == 0 else "dve"

    def wload(W2d, k0, kc, n0, ncols):
        slot = wring.next()
        view = TV(slot.ap[:, 0:kc * ncols].rearrange("p (c n) -> p c n", n=ncols), slot.tok)
        src = W2d[k0 * 128:(k0 + kc) * 128, n0:n0 + ncols].re("(c p) n -> p c n", p=128)
        k.dma("sp", view, src)
        return view

    def linear_fm(W2d, kch, n0, ncols, cb, colw=256):
        nk = len(kch)
        parts = []
        s = 0
        while s < nk:
            e = min(nk, s + 16) if nk <= 16 or nk > 22 else min(nk, s + 11)
            parts.append((s, e))
            s = e
        for g in range(0, ncols, colw):
            w = min(colw, ncols - g)
            tiles = [wload(W2d, s, e - s, n0 + g, w) for (s, e) in parts]
            for m in range(w // 128):
                ps = pmain.next()
                for pi, (s, e) in enumerate(parts):
                    for kk in range(s, e):
                        k.mm(ps, tiles[pi][:, kk - s, m * 128:(m + 1) * 128], kch[kk],
                             start=(kk == 0), stop=(kk == nk - 1))
                cb((n0 + g) // 128 + m, ps)

    def rmsnorm_fm(src, gcol, dst, dst_dt_scale=1.0):
        ps = pmain.next()
        for c in range(NCH):
            sq = sqr.next()
            k.act(sq, src[c], AF.Square)
            k.mm(ps, ones, sq, start=(c == 0), stop=(c == NCH - 1))
        k.act(rstd_t, ps, AF.Ln, bias=EPS, scale=1.0 / D)
        k.act(rstd_t, rstd_t, AF.Exp, scale=-0.5)
        for c in range(NCH):
            k.stt("dve", dst[c], src[c], gcol[:, c:c + 1], rstd_t, ALU.mult, ALU.mult)

    def residual_cb(c, ps):
        k.tt("dve", xfm[c], xfm[c], ps, ALU.add)

    def dump(i):
        if dbg_d is not None:
            for c in range(NCH):
                k.dma("sp", dr(dbg_d[i][:, c * TB:(c + 1) * TB], out_tok), xfm[c])

    def load_tokmajor_to_fm(src2d, ntok_tiles, dst_cb):
        for tt_ in range(ntok_tiles):
            k.dma("sp", xst, dr(src2d[tt_ * 128:(tt_ + 1) * 128, :]))
            for c4 in range(4):
                bank = psm.next()
                for j in range(4):
                    c = c4 * 4 + j
                    k.transpose(q4(bank, j), xst[:, c * 128:(c + 1) * 128], ident)
                eng = evac_eng()
                for j in range(4):
                    dst_cb(c4 * 4 + j, tt_, q4(bank, j), eng)

    def mem_setup(s):
        memf = [TV(t.ap[:, 0:MEM] if TB >= MEM else None, t.tok) for t in xfm]
        assert TB >= MEM

        def cb(c, tt_, ps, eng):
            k.copy(eng, memf[c][:, tt_ * 128:(tt_ + 1) * 128], ps)

        load_tokmajor_to_fm(mem_d[s * MEM:(s + 1) * MEM, :], MEM // 128, cb)
        ps = pmain.next()
        for c in range(NCH):
            sq = sqr.next()
            k.act(sq[:, 0:MEM], memf[c], AF.Square)
            k.mm(ps[:, 0:MEM], ones, sq[:, 0:MEM], start=(c == 0), stop=(c == NCH - 1))
        k.act(rstd_t[:, 0:MEM], ps[:, 0:MEM], AF.Ln, bias=EPS, scale=1.0 / D)
        k.act(rstd_t[:, 0:MEM], rstd_t[:, 0:MEM], AF.Exp, scale=-0.5)
        mh = [t[:, 0:MEM] for t in h]
        for c in range(NCH):
            k.stt("dve", mh[c], memf[c], gprm[:, 16 + c:17 + c], rstd_t[:, 0:MEM], ALU.mult, ALU.mult)
        for l in range(L):
            kvb = None
            def kcb(c, ps, l=l):
                k.copy(evac_eng(), xstb[:, c * MEM:(c + 1) * MEM], ps[:, 0:MEM])
            for g in range(0, D, 256):
                wt = wload(w_xkv_d[l], 0, 16, g, 256)
                for m in range(2):
                    ps = pmain.next()
                    for kk in range(NCH):
                        k.mm(ps[:, 0:MEM], wt[:, kk, m * 128:(m + 1) * 128], mh[kk], start=(kk == 0), stop=(kk == NCH - 1))
                    kcb(g // 128 + m, ps)
            k.dma("sp", dr(kvs_d[l][:, 0:4096], kvs_tok[l]), xstb)
            for g in range(0, D, 256):
                wt = wload(w_xkv_d[l], 0, 16, D + g, 256)
                for mt in range(2):
                    ps = pmain.next()
                    for kk in range(NCH):
                        k.mm(ps[:, 0:256], mh[kk][:, mt * 128:(mt + 1) * 128], wt[:, kk, :], start=(kk == 0), stop=(kk == NCH - 1))
                    k.copy(evac_eng(), xstb[:, mt * 2048 + g: mt * 2048 + g + 256], ps[:, 0:256])
            k.dma("sp", dr(kvs_d[l][:, 4096:8192], kvs_tok[l]), xstb)

    xstb = TV(xst.ap.bitcast(BF16), xst.tok)

    def run_il(gens):
        act_ = list(gens)
        while act_:
            for g_ in list(act_):
                try:
                    next(g_)
                except StopIteration:
                    act_.remove(g_)

    def mixer(l, first_blk):
        P = prm[l]
        rmsnorm_fm(xfm, P[:, P_MIXG:P_MIXG + 16], h)
        Wl = w_in_d[l]
        st8 = {}

        def ba_gen():
            slot = wring.next()
            wba = TV(slot.ap[:, 0:16 * 16].rearrange("p (c n) -> p c n", n=16), slot.tok)
            k.dma("sp", wba, Wl[:, 5120:5136].re("(c p) n -> p c n", p=128))
            psb = pso_full[1][:, 0:TB]
            for kk in range(NCH):
                k.mm(psb[0:16, :], wba[:, kk, :], h[kk], start=(kk == 0), stop=(kk == NCH - 1))
            ba = ba_r.next()
            k.copy("dve", ba, psb[0:16, :])
            yield
            sg_ = ba_r.next()
            k.act(sg_, ba, AF.Exp, scale=-1.0)
            k.ts("dve", sg_, sg_, 1.0, None, ALU.add)
            k.recip(pack[0:16, :], sg_)
            xb = ba_r.next()
            k.ts("dve", xb, ba, P[0:16, P_DTB:P_DTB + 1], None, ALU.add)
            yield
            t1 = ba_r.next()
            k.act(t1, xb, AF.Abs)
            yield
            k.act(t1, t1, AF.Exp, scale=-1.0)
            yield
            k.act(t1, t1, AF.Ln, bias=1.0)
            k.ts("dve", xb, xb, 0.0, None, ALU.max)
            yield
            k.tt("dve", xb, xb, t1, ALU.add)
            yield
            ga = ba_r.next()
            k.ts("dve", ga, xb, nea[l][:, 0:1], None, ALU.mult)
            yield
            gb = ba_r.next()
            cur, nxt = ga, gb
            sh = 1
            while sh < 64:
                cv = cur.re("p (c j) -> p c j", j=64)
                nv = nxt.re("p (c j) -> p c j", j=64)
                k.copy("dve", nv[:, :, 0:sh], cv[:, :, 0:sh])
                k.tt("dve", nv[:, :, sh:64], cv[:, :, sh:64], cv[:, :, 0:64 - sh], ALU.add)
                cur, nxt = nxt, cur
                sh *= 2
                yield
            gc = cur
            k.copy("dve", pack[32:48, :], gc)
            gcv = gc.re("p (c j) -> p c j", j=64)
            dtmp = nxt
            dv = dtmp.re("p (c j) -> p c j", j=64)
            last = TV(gcv.ap[:, :, 63:64].to_broadcast([16, NCK, 64]), gc.tok)
            k.tt("dve", dv, last, gcv, ALU.subtract)
            yield
            k.act(pack[64:80, :], dtmp, AF.Exp)
            egc = ba_r.next()
            k.act(egc, gc, AF.Exp)
            yield
            st8["gc"] = gc
            st8["egc"] = egc

        def colT_emit():
            for t in range(NT):
                ps = q4(pso_full[1], t)
                k.transpose(ps[:, 0:96], pack[0:96, t * 128:(t + 1) * 128], ident[0:96, 0:96])
            for t in range(NT):
                ps = q4(pso_full[1], t)
                k.copy("dve", colT[t], ps[:, 0:96])
                k.ts("dve", ngc[t], colT[t][:, 40:48], -1.0, None, ALU.mult)

        def conv_silu(c, ps, dst, R):
            st = R["stage"].next()
            k.copy("dve", st[:, 0:3], ctail[l][c])
            k.copy("act", st[:, 3:3 + TB], ps)
            yield
            k.copy("dve", ctail[l][c], st[:, TB:TB + 3])
            acc = R["tmpf"].next()
            a = acc[:, 0:TB]
            w = lambda j: P[:, P_DNW + j * 24 + c: P_DNW + j * 24 + c + 1]
            k.act(a, ps, AF.Identity, scale=w(3))
            yield
            k.stt("dve", a, st[:, 2:2 + TB], w(2), a, ALU.mult, ALU.add)
            yield
            k.stt("dve", a, st[:, 1:1 + TB], w(1), a, ALU.mult, ALU.add)
            yield
            k.stt("dve", a, st[:, 0:TB], w(0), a, ALU.mult, ALU.add)
            yield
            k.act(dst, a, AF.Silu)
            yield

        def l2n(src, lnbias, out, R):
            sq = R["sq"]
            k.act(sq, src, AF.Square)
            yield
            ps = R["bank"][:, 0:TB]
            k.mm(ps, ones, sq)
            yield
            rn = R["tmpf"].next()[:, 0:TB]
            k.act(rn, ps, AF.Ln, bias=EPS)
            yield
            k.act(rn, rn, AF.Exp, scale=-0.5, bias=lnbias)
            yield
            out.append(rn)

        qh = AR[0:8]
        kh = AR[8:16]

        ssq16 = pso_full[0][0:16, 0:TB]

        def q_cb(c, ps, R):
            yield from conv_silu(c - 8, ps, qh[c - 8], R)

        def k_cb(c, ps, R):
            yield from conv_silu(c - 8, ps, kh[c - 16], R)

        def v_cb(c, ps, R):
            yield from conv_silu(c - 8, ps, vh[c - 24], R)

        def qk_gen():
            for j in range(16):
                src_ = qh[j] if j < 8 else kh[j - 8]
                sq = sqr.next()
                k.act(sq, src_, AF.Square)
                yield
                k.mm(ssq16, onesel[:, j, :], sq, start=(j == 0), stop=(j == 15))
                yield
            rn16 = ba_r.next()
            k.act(rn16, ssq16, AF.Ln, bias=EPS)
            yield
            k.act(rn16, rn16, AF.Exp, scale=-0.5, bias=cst[0:16, 444:445])
            yield
            bcb = pso_full[0][:, 0:TB]
            trb = TV(pso_full[1].ap.bitcast(BF16), pso_full[1].tok)
            for j in range(16):
                dstl = qh[j] if j < 8 else kh[j - 8]
                k.mm(bcb, sel[:, j * 128:(j + 1) * 128], rn16)
                yield
                k.tt("dve", dstl, dstl, bcb, ALU.mult)
                yield
                if j >= 8:
                    hh = j - 8
                    for t in range(NT):
                        k.transpose(trb[:, t * 128:(t + 1) * 128], dstl[:, t * 128:(t + 1) * 128], identb)
                    yield
                    for t in range(NT):
                        k.ts("dve", kdtok[t][:, hh * 128:(hh + 1) * 128], trb[:, t * 128:(t + 1) * 128],
                             colT[t][:, 72 + hh:73 + hh], None, ALU.mult)
                    yield

        def v_gen():
            for hh in range(8):
                bank = psm.next()
                trv = TV(bank.ap.bitcast(BF16), bank.tok)
                for t in range(NT):
                    k.transpose(trv[:, t * 128:(t + 1) * 128], vh[hh][:, t * 128:(t + 1) * 128], identb)
                yield
                eng = evac_eng()
                for t in range(NT):
                    k.copy(eng, vtok[t][:, hh * 128:(hh + 1) * 128], trv[:, t * 128:(t + 1) * 128])
                yield

        def z_cb(c, ps, R):
            k.act(sz[c - 32], ps, AF.Silu)
            yield

        mixed = AR[16:22] + [sz_extra[0], sz_extra[1]]

        def u_cb(c, ps, R):
            gi = c // 2
            wdw = 2 ** (gi + 1)
            st = R["stage"].next()
            k.copy("dve", st[:, 0:15], ptail[l][c])
            k.copy("act", st[:, 15:15 + TB], ps)
            yield
            k.copy("dve", ptail[l][c], st[:, TB:TB + 15])
            cur = st
            sh = 1
            while sh < wdw:
                nx = R["tmpf"].next()
                lo = 2 * sh - 1
                k.tt("dve", nx[:, lo:15 + TB], cur[:, lo:15 + TB], cur[:, lo - sh:15 + TB - sh], ALU.add)
                cur = nx
                sh *= 2
                yield
            k.stt("dve", mixed[c], cur[:, 15:15 + TB], 1.0 / wdw, st[:, 15:15 + TB], ALU.mult, ALU.subtract)
            yield
            if first_blk:
                tf = R["tmpf"].next()
                k.tt("dve", tf[:, 0:15], cur[:, 15:30], icnt[:, gi * 15:(gi + 1) * 15], ALU.mult)
                yield
                k.tt("dve", mixed[c][:, 0:15], tf[:, 0:15], st[:, 15:30], ALU.subtract)
                yield

        slots = [None] * NSLOT
        slot_i = [0]
        extras = []

        def step_all(n):
            for _ in range(n):
                for i_ in range(NSLOT):
                    g_ = slots[i_]
                    if g_ is not None:
                        try:
                            next(g_)
                        except StopIteration:
                            slots[i_] = None
                for g_ in list(extras):
                    try:
                        next(g_)
                        next(g_)
                    except StopIteration:
                        extras.remove(g_)

        def drain_slot(i_):
            while slots[i_] is not None:
                try:
                    next(slots[i_])
                except StopIteration:
                    slots[i_] = None

        def linear_pipe(n0, ncols, cbgen, nstep=9):
            for g in range(0, ncols, 256):
                wt = wload(Wl, 0, 16, n0 + g, 256)
                for m in range(2):
                    si = slot_i[0] % NSLOT
                    slot_i[0] += 1
                    drain_slot(si)
                    ps = pmain.next()
                    for kk in range(NCH):
                        k.mm(ps, wt[:, kk, m * 128:(m + 1) * 128], h[kk], start=(kk == 0), stop=(kk == NCH - 1))
                    slots[si] = cbgen((n0 + g) // 128 + m, ps, SL[si])
                step_all(nstep)

        def drain_all():
            for i_ in range(NSLOT):
                drain_slot(i_)
            while extras:
                step_all(1)

        extras.append(ba_gen())
        linear_pipe(1024, 1024, q_cb)
        linear_pipe(2048, 1024, k_cb)
        drain_all()
        gc, egc = st8["gc"], st8["egc"]
        colT_emit()
        extras.append(qk_gen())
        linear_pipe(3072, 1024, v_cb)
        for i_ in range(NSLOT):
            drain_slot(i_)
        extras.append(v_gen())
        linear_pipe(4096, 1024, z_cb)
        linear_pipe(0, 1024, u_cb)
        drain_all()
        slot = wring.next()
        wp = TV(slot.ap[:, 0:8 * 256].rearrange("p (c n) -> p c n", n=256), slot.tok)
        k.dma("sp", wp, w_pool_d[l].re("(c p) n -> p c n", p=128))
        cat = h
        for gi in range(4):
            for m in range(2):
                ps = pmain.next()
                for kk in range(2):
                    k.mm(ps, wp[:, gi * 2 + kk, m * 128:(m + 1) * 128], mixed[gi * 2 + kk], start=(kk == 0), stop=(kk == 1))
                c = gi * 2 + m
                k.act(cat[c], ps, AF.Identity, scale=P[:, P_PS + c:P_PS + c + 1])

        for hh in range(8):
            k.copy("act", Sbf[hh], S[l][hh])

        def gdn_pre(t, hh, R):
            cols = slice(t * 128, (t + 1) * 128)
            bka = R["ps"].next()
            bkb = R["ps"].next()
            gB, eB, bB, kk_ps, qk_ps = q4(bka, 0), q4(bka, 1), q4(bka, 2), q4(bka, 3), q4(bkb, 0)
            k.mm(gB, sel[:, (8 + hh) * 128:(9 + hh) * 128], gc[:, cols])
            k.mm(eB, sel[:, (8 + hh) * 128:(9 + hh) * 128], egc[:, cols])
            k.mm(bB, sel[:, hh * 128:(hh + 1) * 128], pack[0:16, cols])
            k.mm(kk_ps, kh[hh][:, cols], kh[hh][:, cols])
            k.mm(qk_ps, kh[hh][:, cols], qh[hh][:, cols])
            yield
            gcol = colT[t][:, 40 + hh:41 + hh]
            ngcol = ngc[t][:, hh:hh + 1]
            bcol = colT[t][:, hh:hh + 1]
            dn = R["sm"].next()
            k.stt("dve", dn, gB, -1.0, Mnat, ALU.mult, ALU.add)
            yield
            k.act(dn, dn, AF.Exp, bias=gcol)
            dT = R["sm"].next()
            k.tt("dve", dT, gB, MT, ALU.add)
            yield
            k.act(dT, dT, AF.Exp, bias=ngcol)
            k.copy("dve", decs[t][hh], TV(eB.ap[:, 63:128:64], eB.tok))
            yield
            k.tt("dve", kgt[t][hh], kh[hh][:, cols], eB, ALU.mult)
            yield
            k.tt("dve", qdt[t][hh], qh[hh][:, cols], eB, ALU.mult)
            yield
            A = R["sm"].next()
            k.stt("dve", A, kk_ps, bcol, dn, ALU.mult, ALU.mult)
            yield
            AT = R["sm"].next()
            k.tt("dve", AT, kk_ps, dT, ALU.mult)
            yield
            k.tt("dve", AT, AT, bB, ALU.mult)
            yield
            at_ = R["sb"].next()
            a1 = R["sm"].next()
            k.tt("dve", a1, dT, ident, ALU.add)
            yield
            k.tt("dve", at_, qk_ps, a1, ALU.mult)
            yield
            T = R["sm"].next()
            k.tt("dve", T, ident, A, ALU.subtract)
            yield
            for it in range(5):
                bk = R["ps"].next()
                pT, pN = q4(bk, 0), q4(bk, 1)
                k.mm(pT, A, AT)
                if it < 4:
                    k.mm(pN, AT, A)
                yield
                A2T = R["sm"].next()
                k.copy("act", A2T, pT)
                if it < 4:
                    A2 = R["sm"].next()
                    k.copy("act", A2, pN)
                yield
                pp = q4(R["ps"].next(), 0)
                k.mm(pp, A2T, T)
                yield
                Tn = R["sm"].next()
                k.tt("dve", Tn, pp, T, ALU.add)
                T = Tn
                if it < 4:
                    A, AT = A2, A2T
                yield
            Tb = R["sb"].next()
            k.copy("dve", Tb, T)
            yield
            bk = R["ps"].next()
            pp1, pp2 = q4(bk, 0), q4(bk, 1)
            k.mm(pp1, Tb, at_)
            k.mm(pp2, Tb, kdtok[t][:, hh * 128:(hh + 1) * 128])
            yield
            k.act(p1t[t][hh], pp1, AF.Identity, scale=bcol)
            k.act(p2t[t][hh], pp2, AF.Identity, scale=bcol)
            yield

        def thread(items, R):
            for (t, hh) in items:
                yield from gdn_pre(t, hh, R)

        items = [(t, hh) for t in range(NT) for hh in range(8)]
        pend_ = [thread(items[i::4], GR[i]) for i in range(4)]
        act_ = []
        rounds_ = 0
        while act_ or pend_:
            if pend_ and rounds_ % 9 == 0:
                act_.append(pend_.pop(0))
            for g_ in list(act_):
                try:
                    next(g_)
                except StopIteration:
                    act_.remove(g_)
            rounds_ += 1

        for t in range(NT):
            cols = slice(t * 128, (t + 1) * 128)
            o_ps = pso
            for ci in range(2):
                rows = slice(ci * 64, (ci + 1) * 64)
                for hh in range(8):
                    bk = scan_ps.next()
                    vS, dS = q4(bk, 0), q4(bk, 1)
                    k.mm(vS[rows, :], kgt[t][hh][:, rows], Sbf[hh], tp=(0, ci * 64))
                    r0 = r0ring.next()
                    k.tt("dve", r0[rows, :], vtok[t][rows, hh * 128:(hh + 1) * 128], vS[rows, :], ALU.subtract)
                    k.mm(o_ps[hh][rows, :], qdt[t][hh][:, rows], Sbf[hh], start=True, stop=False, tp=(0, ci * 64))
                    k.mm(o_ps[hh][rows, :], p1t[t][hh][rows, rows], r0[rows, :], start=False, stop=True, tp=(ci * 64, ci * 64))
                    k.mm(dS, p2t[t][hh][rows, :], r0[rows, :], tp=(ci * 64, 0))
                    k.stt("dve", S[l][hh], S[l][hh], decs[t][hh][:, ci:ci + 1], dS, ALU.mult, ALU.add)
                    k.copy("act", Sbf[hh], S[l][hh])
            ons = [GR[hh % 4]["sm"].next() for hh in range(8)]
            k.memset("dve", ssq8, 0.0)
            for hh in range(8):
                k.act(ons[hh], o_ps[hh], AF.Square, accum=ssq8[:, hh:hh + 1])
            k.act(rs8, ssq8, AF.Ln, bias=EPS, scale=1.0 / 128)
            k.act(rs8, rs8, AF.Exp, scale=-0.5)
            for hh in range(8):
                k.act(ons[hh], o_ps[hh], AF.Identity, scale=rs8[:, hh:hh + 1])
            for hh in range(8):
                pt = q4(scan_ps.next(), 0)
                k.transpose(pt, ons[hh], ident)
                k.stt("dve", cat[8 + hh][:, cols], pt, P[:, P_DNG:P_DNG + 1], sz[hh][:, cols], ALU.mult, ALU.mult)
        linear_fm(w_mo_d[l], cat, 0, D, residual_cb)

    sz_extra = k.tile("szx", [128, 2, TB], BF16, n=2)

    def xattn(l):
        P = prm[l]
        rmsnorm_fm(xfm, P[:, P_XAG:P_XAG + 16], h)
        qT = AR[0:16]

        def q_cb(c, ps):
            k.copy(evac_eng(), qT[c], ps)

        linear_fm(w_xq_d[l], h, 0, D, q_cb)
        s1 = wring.next()
        s2 = wring.next()
        KT = TV(s1.ap.rearrange("p (c m) -> p c m", m=MEM), s1.tok)
        V = TV(s2.ap.rearrange("p (t d) -> p t d", d=D), s2.tok)
        k.dma("sp", s1, dr(kvs_d[l][:, 0:4096], kvs_tok[l]))
        k.dma("sp", s2, dr(kvs_d[l][:, 4096:8192], kvs_tok[l]))
        oT = h
        for a in range(4):
            for mt in range(2):
                ps = pmain.next()
                for dc in range(4):
                    k.mm(ps, KT[:, a * 4 + dc, mt * 128:(mt + 1) * 128], qT[a * 4 + dc], start=(dc == 0), stop=(dc == 3))
                k.act(Ebuf[mt], ps, AF.Exp, scale=float(512.0 ** -0.5))
            den = pmain.next()
            k.mm(den, ones, Ebuf[0], start=True, stop=False)
            k.mm(den, ones, Ebuf[1], start=False, stop=True)
            k.recip(rstd_t, den)
            for dc in range(4):
                ps = pmain.next()
                for mt in range(2):
                    k.mm(ps, V[:, mt, (a * 4 + dc) * 128:(a * 4 + dc + 1) * 128], Ebuf[mt], start=(mt == 0), stop=(mt == 1))
                k.tt("dve", oT[a * 4 + dc], ps, rstd_t, ALU.mult)
        linear_fm(w_xo_d[l], oT, 0, D, residual_cb)

    def ffn(l):
        P = prm[l]
        rmsnorm_fm(xfm, P[:, P_FFG:P_FFG + 16], h)
        hid = AR
        for half in range(2):
            for g in range(11):
                n0 = half * 2816 + g * 256
                wg = wload(w_gate_d[l], 0, 16, n0, 256)
                wu = wload(w_up_d[l], 0, 16, n0, 256)
                for m in range(2):
                    c = n0 // 128 + m
                    cl = g * 2 + m
                    gps = pmain.next()
                    for kk in range(NCH):
                        k.mm(gps, wg[:, kk, m * 128:(m + 1) * 128], h[kk], start=(kk == 0), stop=(kk == NCH - 1))
                    ups = pmain.next()
                    for kk in range(NCH):
                        k.mm(ups, wu[:, kk, m * 128:(m + 1) * 128], h[kk], start=(kk == 0), stop=(kk == NCH - 1))
                    st = SL[0]["stage"].next()
                    k.copy("dve", st[:, 0:2], ftail[l][c])
                    k.copy("act", st[:, 2:2 + TB], gps)
                    k.copy("dve", ftail[l][c], st[:, TB:TB + 2])
                    acc = SL[0]["tmpf"].next()[:, 0:TB]
                    w = lambda j: P[:, P_FCW + j * 44 + c: P_FCW + j * 44 + c + 1]
                    k.act(acc, gps, AF.Identity, scale=w(2), bias=P[:, P_FCB + c:P_FCB + c + 1])
                    k.stt("dve", acc, st[:, 1:1 + TB], w(1), acc, ALU.mult, ALU.add)
                    k.stt("dve", acc, st[:, 0:TB], w(0), acc, ALU.mult, ALU.add)
                    k.act(acc, acc, AF.Silu)
                    k.tt("dve", hid[cl], ups, acc, ALU.mult)
            linear_fm(w_down_d[l][half * 2816:(half + 1) * 2816, :], hid, 0, D, residual_cb)

    for s in range(NSEQ):
        if "mem" in STAGES:
            mem_setup(s)
        for l in range(L):
            for tl in S[l] + ptail[l] + ctail[l] + ftail[l]:
                k.memset("dve", tl, 0.0)
        for b in range(NBLK):
            t0 = (s * NBLK + b) * TB

            def xcb(c, tt_, ps, eng):
                k.copy(eng, xfm[c][:, tt_ * 128:(tt_ + 1) * 128], ps)

            load_tokmajor_to_fm(x_d[t0:t0 + TB, :], NT, xcb)
            for l in range(L):
                if "mixer" in STAGES:
                    mixer(l, b == 0)
                if dbg and s == 0 and b == 0 and l == 0:
                    dump(0)
                if "xattn" in STAGES:
                    xattn(l)
                if dbg and s == 0 and b == 0 and l == 0:
                    dump(1)
                if "ffn" in STAGES:
                    ffn(l)
                if dbg and s == 0 and b == 0 and l == 0:
                    dump(2)
            if final:
                ps = pmain.next()
                for c in range(NCH):
                    sq = sqr.next()
                    k.act(sq, xfm[c], AF.Square)
                    k.mm(ps, ones, sq, start=(c == 0), stop=(c == NCH - 1))
                k.act(rstd_t, ps, AF.Ln, bias=EPS, scale=1.0 / D)
                k.act(rstd_t, rstd_t, AF.Exp, scale=-0.5)
                for c in range(NCH):
                    k.stt("dve", xfm[c], xfm[c], gprm[:, c:c + 1], rstd_t, ALU.mult, ALU.mult)
            for tt_ in range(NT):
                for c in range(NCH):
                    pt = q4(psm.next(), 0)
                    k.transpose(pt, xfm[c][:, tt_ * 128:(tt_ + 1) * 128], ident)
                    k.copy(evac_eng(), xst[:, c * 128:(c + 1) * 128], pt)
                k.dma("sp", dr(out_d[t0 + tt_ * 128:t0 + (tt_ + 1) * 128, :], out_tok), xst)
    k.finalize()
    return nc, k


def make_consts():
    cst = np.zeros((128, 448), np.float32)
    cst[:, 0:128] = np.eye(128, dtype=np.float32)
    p = np.arange(128)[:, None]
    f = np.arange(128)[None, :]
    same = (p // 64) == (f // 64)
    cst[:, 128:256] = np.where(same & (p > f), 0.0, NEG)
    cst[:, 256:384] = np.where(same & (f > p), 0.0, NEG)
    for gi in range(4):
        w = 2 ** (gi + 1)
        cnt = np.minimum(np.arange(1, 16), w).astype(np.float32)
        cst[:, 384 + gi * 15:384 + (gi + 1) * 15] = (1.0 / cnt)[None, :]
    cst[0:8, 444] = np.log(128.0 ** -0.5)
    sel = np.zeros((16, 16, 128), np.float32)
    for j in range(16):
        sel[j, j, :] = 1.0
    return cst, sel.reshape(16, 16 * 128)


def colmajor(v):
    return np.ascontiguousarray(v.reshape(-1, 128).T)


def pack_params(inp, l):
    P = np.zeros((128, NCOL), np.float32)
    P[:, P_MIXG:P_MIXG + 16] = colmajor(inp["mix_norm_g"][l])
    P[:, P_XAG:P_XAG + 16] = colmajor(inp["xa_norm_g"][l])
    P[:, P_FFG:P_FFG + 16] = colmajor(inp["ffn_norm_g"][l])
    P[:, P_PS:P_PS + 8] = colmajor(inp["pool_scale"][l])
    for j in range(4):
        P[:, P_DNW + j * 24:P_DNW + (j + 1) * 24] = colmajor(inp["dn_conv_w"][l][j])
    P[:, P_DNG] = inp["dn_norm_g"][l]
    for j in range(3):
        P[:, P_FCW + j * 44:P_FCW + (j + 1) * 44] = colmajor(inp["ffn_conv_w"][l][j])
    P[:, P_FCB:P_FCB + 44] = colmajor(inp["ffn_conv_b"][l])
    P[8:16, P_DTB] = inp["dn_dt_bias"][l]
    P[8:16, P_ALOG] = inp["dn_a_log"][l]
    return P


_CACHE = {}


def get_program(NSEQ, NBLK, TB, L, final, dbg=False):
    key = (NSEQ, NBLK, TB, L, final, dbg)
    if key not in _CACHE:
        _CACHE[key] = build_program(NSEQ, NBLK, TB, L, final, dbg)[0]
    return _CACHE[key]


TB_DEFAULT = 256
KVS_KIND = "Internal"
NPSM = 3
SILU_ENG = "pool"
TAP_ENG = "pool"
STAGES = {"mem", "mixer", "xattn", "ffn"}
WNAMES = ["w_in", "w_mix_out", "w_xq", "w_xkv", "w_xo", "w_gate", "w_up", "w_down"]


def kernel(**inp):
    inp = {k_: np.asarray(v) for k_, v in inp.items()}
    n = 8
    B = inp["x"].shape[0]
    NSEQ = B // n
    L = inp["w_in"].shape[0]
    TB = TB_DEFAULT
    nc = get_program(NSEQ, SEQ // TB, TB, L, True)
    cst, sel = make_consts()
    prm = np.stack([pack_params(inp, l) for l in range(L)])
    gprm = np.concatenate([colmajor(inp["final_norm_g"]), colmajor(inp["mem_norm_g"])], axis=1)
    shared = {w: np.ascontiguousarray(inp[w], dtype=np.float32) for w in WNAMES}
    shared["w_pool"] = np.ascontiguousarray(inp["w_pool"].reshape(L, 1024, 256))
    shared.update(prm=prm, gprm=np.ascontiguousarray(gprm), cst=cst, sel=sel)
    in_maps = []
    for c in range(n):
        m = dict(shared)
        m["x"] = np.ascontiguousarray(inp["x"][c * NSEQ:(c + 1) * NSEQ].reshape(NSEQ * SEQ, D))
        m["mem"] = np.ascontiguousarray(inp["mem"][c * NSEQ:(c + 1) * NSEQ].reshape(NSEQ * MEM, D))
        in_maps.append(m)
    res = run_bass_kernel_spmd(nc, in_maps, core_ids=list(range(n)))
    out = np.concatenate([r["out"].reshape(NSEQ, SEQ, D) for r in res.results], axis=0)
    return out.astype(np.float32)
```
